# Optimizing a Trainium2 kernel written in Bass

```python
import math
import jax, jax.numpy as jnp
from jax import lax
import numpy as np

D_MODEL = 2048
BATCH = 8
SEQ = 2048
DEPTH = 2

N_EVEN = (DEPTH + 1) // 2
N_ODD = DEPTH // 2

D_FF = 256 * ((8 * D_MODEL + 3 * 256 - 1) // (3 * 256))

A_WIDTH = D_MODEL
A_CHUNK = 128
A_GROUP = 128
A_HEADS = A_WIDTH // A_GROUP

B_HEAD_DIM = 64
B_WIDTH = D_MODEL
B_HEADS = B_WIDTH // B_HEAD_DIM
B_GROUPS = 4
B_HPG = B_HEADS // B_GROUPS
B_STATE = 128
B_CONV = 4
B_CHUNK = 128
B_CONV_CH = B_WIDTH + 2 * B_GROUPS * B_STATE

AB_IN = 2 * A_WIDTH + B_WIDTH + B_CONV_CH + B_HEADS
AB_MIX = A_WIDTH + B_WIDTH

ATT_HEAD_DIM = 128
N_ATT_HEADS = D_MODEL // ATT_HEAD_DIM
D_HEADS = N_ATT_HEADS // 4
C_HEADS = N_ATT_HEADS - D_HEADS
C_WIDTH = C_HEADS * ATT_HEAD_DIM
D_WIDTH = D_HEADS * ATT_HEAD_DIM
CD_IN = 3 * (C_WIDTH + D_WIDTH)
CD_MIX = C_WIDTH + D_WIDTH
C_QBLOCK = 128
MOBA_BLOCK = 256
MOBA_TOPK = 3
MOBA_QSUB = 32

kernel_name = "hybrid_gmlp_ssd_stickbreak_moba_block"


def rms_norm(x, g, eps=1e-6):
    xf = x.astype(jnp.float32)
    y = xf * lax.rsqrt(jnp.mean(xf * xf, axis=-1, keepdims=True) + eps)
    return (y * g.astype(jnp.float32)).astype(x.dtype)


def layer_norm(x, g, b, eps=1e-5):
    xf = x.astype(jnp.float32)
    mu = jnp.mean(xf, axis=-1, keepdims=True)
    var = jnp.mean(jnp.square(xf - mu), axis=-1, keepdims=True)
    y = (xf - mu) * lax.rsqrt(var + eps) * g.astype(jnp.float32) + b.astype(jnp.float32)
    return y.astype(x.dtype)


def swiglu(h, w_gate, w_up, w_down):
    a = jax.nn.silu(jnp.einsum('bsd,df->bsf', h, w_gate)) * jnp.einsum('bsd,df->bsf', h, w_up)
    return jnp.einsum('bsf,fd->bsd', a, w_down)


def chunked_spatial_gating(u, v, ln_g, ln_b, w_s, b_s):
    bsz, s, _ = v.shape
    n_chunks = s // A_CHUNK
    v = layer_norm(v, ln_g, ln_b).reshape(bsz, n_chunks, A_CHUNK, A_HEADS, A_GROUP)
    causal = jnp.tril(jnp.ones((A_CHUNK, A_CHUNK), dtype=bool))
    w = jnp.where(causal, w_s, 0.0).astype(v.dtype)
    mixed = jnp.einsum('hts,bcshe->bcthe', w, v) + b_s.T.astype(v.dtype)[:, :, None]
    return u * mixed.reshape(bsz, s, A_WIDTH)


def causal_depthwise_conv(x, w, b):
    k_w, ch = w.shape
    y = lax.conv_general_dilated(
        x, w[:, None, :].astype(x.dtype), window_strides=(1,), padding=[(k_w - 1, 0)],
        dimension_numbers=('NWC', 'WIO', 'NWC'), feature_group_count=ch)
    return y + b.astype(x.dtype)


def ssd_scan(x, dt, a, b_mat, c_mat):
    bsz, s, g, r, p = x.shape
    n = b_mat.shape[-1]
    nc, L = s // B_CHUNK, B_CHUNK
    xc = (x * dt[..., None]).reshape(bsz, nc, L, g, r, p)
    a_dt = (dt * a).reshape(bsz, nc, L, g, r).transpose(0, 1, 3, 4, 2)
    bc = b_mat.reshape(bsz, nc, L, g, n)
    cc = c_mat.reshape(bsz, nc, L, g, n)
    a_cum = jnp.cumsum(a_dt, axis=-1)
    causal = jnp.tril(jnp.ones((L, L), dtype=bool))
    seg = a_cum[..., :, None] - a_cum[..., None, :]
    decay = jnp.exp(jnp.where(causal, seg, -jnp.inf))
    cb = jnp.einsum('bctgn,bcsgn->bcgts', cc, bc)
    y_diag = jnp.einsum('bcgrts,bcsgrp->bctgrp', cb[:, :, :, None] * decay, xc)
    decay_to_end = jnp.exp(a_cum[..., -1:] - a_cum)
    states = jnp.einsum('bcsgn,bcgrs,bcsgrp->bcgrpn', bc, decay_to_end, xc)
    chunk_decay = jnp.exp(a_cum[..., -1])

    def step(h, inp):
        s_c, d_c = inp
        return h * d_c[..., None, None] + s_c, h

    h0 = jnp.zeros((bsz, g, r, p, n), jnp.float32)
    _, prev = lax.scan(step, h0, (jnp.moveaxis(states, 1, 0), jnp.moveaxis(chunk_decay, 1, 0)))
    prev = jnp.moveaxis(prev, 0, 1)
    y_off = jnp.einsum('bctgn,bcgrpn,bcgrt->bctgrp', cc, prev, jnp.exp(a_cum))
    return (y_diag + y_off).reshape(bsz, s, g, r, p)


def ssd_mixer(z, xbc, dt_raw, conv_w, conv_b, dt_bias, a_log, d_skip, norm_g):
    bsz, s, _ = z.shape
    f32 = jnp.float32
    xbc = jax.nn.silu(causal_depthwise_conv(xbc, conv_w, conv_b))
    xs, bm, cm = jnp.split(xbc, [B_WIDTH, B_WIDTH + B_GROUPS * B_STATE], axis=-1)
    xs = xs.astype(f32).reshape(bsz, s, B_GROUPS, B_HPG, B_HEAD_DIM)
    bm = bm.astype(f32).reshape(bsz, s, B_GROUPS, B_STATE)
    cm = cm.astype(f32).reshape(bsz, s, B_GROUPS, B_STATE)
    dt = jax.nn.softplus(dt_raw.astype(f32) + dt_bias.astype(f32)).reshape(bsz, s, B_GROUPS, B_HPG)
    a = -jnp.exp(a_log.astype(f32)).reshape(B_GROUPS, B_HPG)
    y = ssd_scan(xs, dt, a, bm, cm) + d_skip.astype(f32).reshape(B_GROUPS, B_HPG)[..., None] * xs
    y = y.reshape(bsz, s, B_WIDTH).astype(z.dtype)
    return rms_norm(y * jax.nn.silu(z), norm_g)


def stick_breaking_attention(q, k, v):
    bsz, s, h, dh = q.shape
    scale = dh ** -0.5
    outs = []
    for i in range(s // C_QBLOCK):
        q0 = i * C_QBLOCK
        kv_len = q0 + C_QBLOCK
        logits = jnp.einsum('bqhd,bkhd->bhqk', q[:, q0:kv_len], k[:, :kv_len]).astype(jnp.float32) * scale
        q_pos = q0 + jnp.arange(C_QBLOCK)
        k_pos = jnp.arange(kv_len)
        past = k_pos[None, :] < q_pos[:, None]
        log_beta = jax.nn.log_sigmoid(logits)
        log_keep = jnp.where(past, jax.nn.log_sigmoid(-logits), 0.0)
        log_w = log_beta + lax.cumsum(log_keep, axis=3, reverse=True) - log_keep
        w = jnp.where(past, jnp.exp(log_w), 0.0).astype(v.dtype)
        outs.append(jnp.einsum('bhqk,bkhd->bqhd', w, v[:, :kv_len]))
    return jnp.concatenate(outs, axis=1)


def moba_attention(q, k, v):
    bsz, s, h, dh = q.shape
    f32 = jnp.float32
    scale = dh ** -0.5
    n_blk = -(-s // MOBA_BLOCK)
    s_pad = n_blk * MOBA_BLOCK
    pad = ((0, 0), (0, s_pad - s), (0, 0), (0, 0))
    q, k, v = jnp.pad(q, pad), jnp.pad(k, pad), jnp.pad(v, pad)
    kb = k.reshape(bsz, n_blk, MOBA_BLOCK, h, dh)
    vb = v.reshape(bsz, n_blk, MOBA_BLOCK, h, dh)
    k_mean = jnp.mean(kb.astype(f32), axis=2)
    gate = jnp.einsum('bshd,bnhd->bshn', q.astype(f32), k_mean)
    q_blk = jnp.arange(s_pad) // MOBA_BLOCK
    fully_past = jnp.arange(n_blk)[None, :] < q_blk[:, None]
    gate = jnp.where(fully_past[None, :, None, :], gate, -jnp.inf)
    n_sel = min(MOBA_TOPK, n_blk)
    _, sel = lax.top_k(gate, n_sel)
    sel_ok = sel < q_blk[None, :, None, None]
    kbh = kb.transpose(0, 3, 1, 2, 4)
    vbh = vb.transpose(0, 3, 1, 2, 4)
    b_ix = jnp.arange(bsz)[:, None, None, None]
    h_ix = jnp.arange(h)[None, None, :, None]

    def sub_block(n):
        start = n * MOBA_QSUB
        q_n = lax.dynamic_slice_in_dim(q, start, MOBA_QSUB, axis=1)
        sel_n = lax.dynamic_slice_in_dim(sel, start, MOBA_QSUB, axis=1)
        ok_n = lax.dynamic_slice_in_dim(sel_ok, start, MOBA_QSUB, axis=1)
        own_start = (start // MOBA_BLOCK) * MOBA_BLOCK
        k_own = lax.dynamic_slice_in_dim(k, own_start, MOBA_BLOCK, axis=1)
        v_own = lax.dynamic_slice_in_dim(v, own_start, MOBA_BLOCK, axis=1)
        k_sel = kbh[b_ix, h_ix, sel_n]
        v_sel = vbh[b_ix, h_ix, sel_n]
        s_sel = jnp.einsum('bqhd,bqhkcd->bqhkc', q_n, k_sel).astype(f32) * scale
        s_sel = jnp.where(ok_n[..., None], s_sel, -jnp.inf).reshape(bsz, MOBA_QSUB, h, n_sel * MOBA_BLOCK)
        q_pos = start + jnp.arange(MOBA_QSUB)
        k_pos = own_start + jnp.arange(MOBA_BLOCK)
        causal = k_pos[None, :] <= q_pos[:, None]
        s_own = jnp.einsum('bqhd,bchd->bqhc', q_n, k_own).astype(f32) * scale
        s_own = jnp.where(causal[None, :, None, :], s_own, -jnp.inf)
        p = jax.nn.softmax(jnp.concatenate([s_sel, s_own], axis=-1), axis=-1).astype(v.dtype)
        p_sel = p[..., :n_sel * MOBA_BLOCK].reshape(bsz, MOBA_QSUB, h, n_sel, MOBA_BLOCK)
        p_own = p[..., n_sel * MOBA_BLOCK:]
        return (jnp.einsum('bqhkc,bqhkcd->bqhd', p_sel, v_sel)
                + jnp.einsum('bqhc,bchd->bqhd', p_own, v_own))

    out = lax.map(sub_block, jnp.arange(s_pad // MOBA_QSUB))
    out = jnp.moveaxis(out, 0, 1).reshape(bsz, s_pad, h, dh)
    return out[:, :s]


def ab_mixer(hm, w_in, w_out, gm_ln_g, gm_ln_b, gm_w_s, gm_b_s,
             conv_w, conv_b, dt_bias, a_log, d_skip, ssd_norm_g):
    proj = jnp.einsum('bsd,de->bse', hm, w_in)
    u, v, z, xbc, dt_raw = jnp.split(
        proj, [A_WIDTH, 2 * A_WIDTH, 2 * A_WIDTH + B_WIDTH, 2 * A_WIDTH + B_WIDTH + B_CONV_CH], axis=-1)
    y_a = chunked_spatial_gating(jax.nn.gelu(u), jax.nn.gelu(v), gm_ln_g, gm_ln_b, gm_w_s, gm_b_s)
    y_b = ssd_mixer(z, xbc, dt_raw, conv_w, conv_b, dt_bias, a_log, d_skip, ssd_norm_g)
    return jnp.einsum('bse,ed->bsd', jnp.concatenate([y_a, y_b], axis=-1), w_out)


def cd_mixer(hm, w_in, w_out, q_norm_g, k_norm_g):
    bsz, s, _ = hm.shape
    proj = jnp.einsum('bsd,de->bse', hm, w_in)
    qc, kc, vc, qd, kd, vd = jnp.split(
        proj, [C_WIDTH, 2 * C_WIDTH, 3 * C_WIDTH, 3 * C_WIDTH + D_WIDTH, 3 * C_WIDTH + 2 * D_WIDTH], axis=-1)
    heads_c = lambda t: t.reshape(bsz, s, C_HEADS, ATT_HEAD_DIM)
    heads_d = lambda t: t.reshape(bsz, s, D_HEADS, ATT_HEAD_DIM)
    y_c = stick_breaking_attention(heads_c(qc), heads_c(kc), heads_c(vc))
    y_d = moba_attention(rms_norm(heads_d(qd), q_norm_g), rms_norm(heads_d(kd), k_norm_g), heads_d(vd))
    y = jnp.concatenate([y_c.reshape(bsz, s, C_WIDTH), y_d.reshape(bsz, s, D_WIDTH)], axis=-1)
    return jnp.einsum('bse,ed->bsd', y, w_out)


def setup_inputs(seed: int = 0) -> dict:
    key = jax.random.key(seed)
    ks = iter(jax.random.split(key, 40))
    f32 = jnp.float32

    def nrm(shape, scale):
        return jax.random.normal(next(ks), shape, f32) * scale

    E, O = N_EVEN, N_ODD
    x = nrm((BATCH, SEQ, D_MODEL), 1.0)
    c = nrm((BATCH, D_MODEL), 1.0)
    norm_mix_g = 1.0 + nrm((DEPTH, D_MODEL), 0.05)
    norm_ffn_g = 1.0 + nrm((DEPTH, D_MODEL), 0.05)
    ada_w = nrm((DEPTH, D_MODEL, 6 * D_MODEL), 0.5 * D_MODEL ** -0.5)
    ada_b = nrm((DEPTH, 6 * D_MODEL), 0.05)
    ffn_w_gate = nrm((DEPTH, D_MODEL, D_FF), D_MODEL ** -0.5)
    ffn_w_up = nrm((DEPTH, D_MODEL, D_FF), D_MODEL ** -0.5)
    ffn_w_down = nrm((DEPTH, D_FF, D_MODEL), D_FF ** -0.5)
    ab_w_in = nrm((E, D_MODEL, AB_IN), D_MODEL ** -0.5)
    ab_w_out = nrm((E, AB_MIX, D_MODEL), AB_MIX ** -0.5)
    gm_ln_g = 1.0 + nrm((E, A_WIDTH), 0.05)
    gm_ln_b = nrm((E, A_WIDTH), 0.02)
    gm_w_s = nrm((E, A_HEADS, A_CHUNK, A_CHUNK), A_CHUNK ** -0.5)
    gm_b_s = 1.0 + nrm((E, A_HEADS, A_CHUNK), 0.05)
    ssd_conv_w = nrm((E, B_CONV, B_CONV_CH), B_CONV ** -0.5)
    ssd_conv_b = nrm((E, B_CONV_CH), 0.02)
    dt0 = jnp.exp(jax.random.uniform(next(ks), (E, B_HEADS), f32,
                                     minval=math.log(1e-3), maxval=math.log(1e-1)))
    ssd_dt_bias = dt0 + jnp.log(-jnp.expm1(-dt0))
    ssd_a_log = jnp.log(jax.random.uniform(next(ks), (E, B_HEADS), f32, minval=1.0, maxval=16.0))
    ssd_d = 1.0 + nrm((E, B_HEADS), 0.1)
    ssd_norm_g = 1.0 + nrm((E, B_WIDTH), 0.05)
    cd_w_in = nrm((O, D_MODEL, CD_IN), D_MODEL ** -0.5)
    cd_w_out = nrm((O, CD_MIX, D_MODEL), CD_MIX ** -0.5)
    moba_q_norm_g = 1.0 + nrm((O, ATT_HEAD_DIM), 0.05)
    moba_k_norm_g = 1.0 + nrm((O, ATT_HEAD_DIM), 0.05)
    return {"x": x, "c": c, "norm_mix_g": norm_mix_g, "norm_ffn_g": norm_ffn_g,
            "ada_w": ada_w, "ada_b": ada_b, "ffn_w_gate": ffn_w_gate, "ffn_w_up": ffn_w_up,
            "ffn_w_down": ffn_w_down, "ab_w_in": ab_w_in, "ab_w_out": ab_w_out,
            "gm_ln_g": gm_ln_g, "gm_ln_b": gm_ln_b, "gm_w_s": gm_w_s, "gm_b_s": gm_b_s,
            "ssd_conv_w": ssd_conv_w, "ssd_conv_b": ssd_conv_b, "ssd_dt_bias": ssd_dt_bias,
            "ssd_a_log": ssd_a_log, "ssd_d": ssd_d, "ssd_norm_g": ssd_norm_g,
            "cd_w_in": cd_w_in, "cd_w_out": cd_w_out,
            "moba_q_norm_g": moba_q_norm_g, "moba_k_norm_g": moba_k_norm_g}


def reference(x, c, norm_mix_g, norm_ffn_g, ada_w, ada_b, ffn_w_gate, ffn_w_up, ffn_w_down,
              ab_w_in, ab_w_out, gm_ln_g, gm_ln_b, gm_w_s, gm_b_s,
              ssd_conv_w, ssd_conv_b, ssd_dt_bias, ssd_a_log, ssd_d, ssd_norm_g,
              cd_w_in, cd_w_out, moba_q_norm_g, moba_k_norm_g):
    c_act = jax.nn.silu(c)
    for layer in range(DEPTH):
        mod = (c_act @ ada_w[layer] + ada_b[layer])[:, None, :]
        sh_m, sc_m, g_m, sh_f, sc_f, g_f = jnp.split(mod, 6, axis=-1)
        hm = rms_norm(x, norm_mix_g[layer]) * (1.0 + sc_m) + sh_m
        if layer % 2 == 0:
            e = layer // 2
            y = ab_mixer(hm, ab_w_in[e], ab_w_out[e], gm_ln_g[e], gm_ln_b[e], gm_w_s[e], gm_b_s[e],
                         ssd_conv_w[e], ssd_conv_b[e], ssd_dt_bias[e], ssd_a_log[e], ssd_d[e],
                         ssd_norm_g[e])
        else:
            o = layer // 2
            y = cd_mixer(hm, cd_w_in[o], cd_w_out[o], moba_q_norm_g[o], moba_k_norm_g[o])
        x = x + (1.0 + g_m) * y
        hf = rms_norm(x, norm_ffn_g[layer]) * (1.0 + sc_f) + sh_f
        x = x + (1.0 + g_f) * swiglu(hf, ffn_w_gate[layer], ffn_w_up[layer], ffn_w_down[layer])
    return x
```

```python
import math
from contextlib import ExitStack

import numpy as np
import concourse.bass as bass
import concourse.mybir as mybir
from concourse.bass_utils import run_bass_kernel_spmd

F32 = mybir.dt.float32
BF16 = mybir.dt.bfloat16
AF = mybir.ActivationFunctionType
ALU = mybir.AluOpType
AX = mybir.AxisListType

ENGS = ("pe", "act", "dve", "pool", "sp")
P = 128
S = 2048
D = 2048
DC = 16
DFF = 5632
FC = 44
ABIN = 9248


class Sem:
    def __init__(self, handle, name):
        self.h = handle
        self.name = name
        self.val = 0


class Trk:
    __slots__ = ("w", "r")

    def __init__(self):
        self.w = None
        self.r = []


class Tile:
    def __init__(self, t):
        self.t = t
        self.k = Trk()

    def __getitem__(self, idx):
        return self.t[idx]


class KB:
    def __init__(self, nc, stack):
        self.nc = nc
        self.stack = stack
        self.eng = {"pe": nc.tensor, "act": nc.scalar, "dve": nc.vector,
                    "pool": nc.gpsimd, "sp": nc.sync}
        self.all_sems = []
        self.esem = {e: self.sem("e_" + e) for e in ENGS}
        self.seen = {e: {} for e in ENGS}
        self.n_inst = 0

    def sem(self, name):
        h = self.stack.enter_context(self.nc.semaphore(name))
        s = Sem(h, name)
        self.all_sems.append(s)
        return s

    def _wait(self, e, sem, val):
        if getattr(sem, "shared", False):
            val = sem.val
        seen = self.seen[e]
        if seen.get(sem, 0) >= val:
            return
        seen[sem] = val
        self.eng[e].wait_ge(sem.h, val)

    def _deps(self, e, reads, writes):
        for t in reads:
            if t.w is not None:
                s, v, we = t.w
                self._wait(e, s, v)
        for t in writes:
            if t.w is not None:
                s, v, we = t.w
                if not (e == "pe" and we == "pe"):
                    self._wait(e, s, v)
            for (s, v, re) in t.r:
                if re == e and s is self.esem[e]:
                    continue
                self._wait(e, s, v)

    def _mark(self, e, sem, val, reads, writes):
        for t in reads:
            t.r.append((sem, val, e))
            if len(t.r) > 64:
                t.r = t.r[-48:]
        for t in writes:
            t.w = (sem, val, e)
            t.r = []

    def op(self, e, fn, reads=(), writes=()):
        self._deps(e, reads, writes)
        inst = fn(self.eng[e])
        s = self.esem[e]
        s.val += 1
        inst.then_inc(s.h, 1)
        self._mark(e, s, s.val, reads, writes)
        self.n_inst += 1
        return inst

    def dma(self, e, out, in_, sem, reads=(), writes=()):
        self._deps(e, reads, writes)
        inst = self.eng[e].dma_start(out=out, in_=in_)
        sem.val += 16
        inst.then_inc(sem.h, 16)
        self._mark(e, sem, sem.val, reads, writes)
        self.n_inst += 1
        return inst

    def barrier(self):
        for e in ENGS:
            for s in self.all_sems:
                if s.val > 0:
                    self._wait(e, s, s.val)


class Ring:
    def __init__(self, kb, alloc, n, shape, dt, name, sems=None):
        self.slots = [Tile(alloc(shape, dt, f"{name}{i}")) for i in range(n)]
        self.sems = sems if sems is not None else [kb.sem(f"{name}_s{i}") for i in range(n)]
        self.i = 0
        self.n = n

    def next(self):
        s = self.slots[self.i % self.n]
        sem = self.sems[self.i % self.n]
        self.i += 1
        return s, sem


def build_program(dbg=None, stop_after=None, skip=(), e0step=99, nchunk=16):
    dbg = dbg or ()
    nc = bass.Bass("TRN2", target_bir_lowering=False)

    def din(name, shape, dt=F32):
        return nc.dram_tensor(name, list(shape), dt, kind="ExternalInput").ap()

    def dscr(name, shape, dt):
        kind = "ExternalOutput" if name in dbg else "Internal"
        return nc.dram_tensor(name, list(shape), dt, kind=kind).ap()

    xT = din("xT", [D, S])
    c_col = din("c_col", [P, DC])
    ada_w = din("ada_w", [2, D, 6 * D])
    ada_b = din("ada_b", [2, P, 96])
    gmix = din("gmix", [2, P, DC])
    gffn = din("gffn", [2, P, DC])
    w_gate = din("ffn_w_gate", [2, D, DFF])
    w_up = din("ffn_w_up", [2, D, DFF])
    w_down = din("ffn_w_down", [2, DFF, D])
    ab_w_in = din("ab_w_in", [D, ABIN])
    ab_w_out = din("ab_w_out", [2 * D, D])
    cd_w_in = din("cd_w_in", [D, 3 * D])
    cd_w_out = din("cd_w_out", [D, D])
    lng = din("lng", [P, DC])
    lnb = din("lnb", [P, DC])
    wsT_in = din("wsT", [P, DC * P])
    bsbc_in = din("bsbc", [P, DC * P])
    cw_in = din("cw", [P, 24 * 4])
    cb_in = din("cb", [P, 24])
    dtb_in = din("dtb", [P, 32])
    alog_in = din("alog", [P, 32])
    dbc_in = din("dbc", [P, D])
    sgbc_in = din("sgbc", [P, D])
    gq_col = din("gq_col", [P, 1])
    gk_col = din("gk_col", [P, 1])
    gq_bc = din("gq_bc", [P, P])
    gk_bc = din("gk_bc", [P, P])
    cst_in = din("cst", [P, 5 * P + 8 * 512 + 3 * 128])
    outT = nc.dram_tensor("outT", [D, S], F32, kind="ExternalOutput").ap()

    UT = dscr("UT", [D, S], BF16)
    VTOK = dscr("VTOK", [S, D], F32)
    SZ = dscr("SZ", [S, D], F32)
    XBCT = dscr("XBCT", [3072, S], F32)
    XCT = dscr("XCT", [3072, S], F32)
    DTs = dscr("DT", [S, 32], F32)
    MIX = dscr("MIX", [2 * D, S], BF16)
    X1 = dscr("X1", [D, S], F32)
    X2 = dscr("X2", [D, S], F32)
    AT = dscr("AT", [DFF, S], BF16)
    QT = dscr("QT", [D, S], BF16)
    KT = dscr("KT", [D, S], BF16)
    VT = dscr("VT", [S, D], BF16)
    MODD = dscr("MODD", [2, P, 96], F32)

    top = ExitStack()
    with top:
        kb = KB(nc, top)

        uid = [0]

        def mk_alloc(st):
            def sb(shape, dt, name):
                uid[0] += 1
                return st.enter_context(nc.sbuf_tensor(f"sb{uid[0]}_{name}", list(shape), dt))
            return sb

        gsb = mk_alloc(top)

        cst = Tile(gsb([P, 5 * P + 3 * 128], F32, "cst"))
        ident = cst[:, 0:128]
        tri_le = cst[:, 128:256]
        sl_gt = cst[:, 256:384]
        tri_ge = cst[:, 384:512]
        ones32 = cst[:, 512:640]
        o1 = 640
        MS0 = 1024
        negmask = cst[:, o1:o1 + 128].rearrange("p (t n) -> p t n", n=8)
        validm = cst[:, o1 + 128:o1 + 256].rearrange("p (t n) -> p t n", n=8)
        ownm = cst[:, o1 + 256:o1 + 384].rearrange("p (t n) -> p t n", n=8)
        ones_bf = Tile(gsb([P, P], BF16, "ones_bf"))
        trige_bf = Tile(gsb([P, P], BF16, "trige_bf"))
        oh_bf = Tile(gsb([P, 8 * P], BF16, "oh_bf"))
        modv = [Tile(gsb([P, 96], F32, f"mod{l}")) for l in range(2)]
        A1 = [Tile(gsb([P, DC], F32, f"A1_{l}")) for l in range(2)]
        G1 = [Tile(gsb([P, DC], F32, f"G1_{l}")) for l in range(2)]
        A2 = [Tile(gsb([P, DC], F32, f"A2_{l}")) for l in range(2)]
        G2 = [Tile(gsb([P, DC], F32, f"G2_{l}")) for l in range(2)]
        gm_t = Tile(gsb([P, 2 * DC], F32, "gm_t"))
        gf_t = Tile(gsb([P, 2 * DC], F32, "gf_t"))
        adab_t = Tile(gsb([P, 2 * 96], F32, "adab_t"))
        ccol = Tile(gsb([P, DC], F32, "ccol"))
        cact2 = Tile(gsb([P, DC, 2], F32, "cact2"))

        banks = [Tile(top.enter_context(nc.psum_tensor(f"bank{i}", [P, 512], F32))) for i in range(8)]
        bank_i = [0]

        def nbank():
            b = banks[bank_i[0] % 8]
            bank_i[0] += 1
            return b

        rot6 = [0]

        def nbank6():
            b = banks[2 + rot6[0] % 6]
            rot6[0] += 1
            return b

        dsems = [kb.sem(f"d{i}") for i in range(16)]
        wring = Ring(kb, gsb, 3, [P, 8192], BF16, "wr")
        stg = Ring(kb, gsb, 4, [P, 512], F32, "stg")
        stgb = Ring(kb, gsb, 4, [P, 512], BF16, "stgb")
        misc_sem = kb.sem("misc")
        misc_sem.shared = True

        def mm(out, lhsT, rhs, start, stop, reads, writes):
            kb.op("pe", lambda e: e.matmul(out, lhsT, rhs, start=start, stop=stop), reads, writes)

        def tp(out, in_, reads, writes):
            kb.op("pe", lambda e: e.matmul(out, in_, ident, start=True, stop=True), list(reads) + [cst.k], writes)

        def act(out, in_, func, reads, writes, bias=None, scale=None, accum=None):
            kw = {}
            if bias is not None:
                kw["bias"] = bias
            if scale is not None:
                kw["scale"] = scale
            if accum is not None:
                kw["accum_out"] = accum
            kb.op("act", lambda e: e.activation(out=out, in_=in_, func=func, **kw), reads, writes)

        def tt(eng, out, in0, in1, op, reads, writes):
            kb.op(eng, lambda e: e.tensor_tensor(out=out, in0=in0, in1=in1, op=op), reads, writes)

        def ts(eng, out, in0, s1, s2, op0, op1, reads, writes):
            if s2 is None:
                kb.op(eng, lambda e: e.tensor_scalar(out=out, in0=in0, scalar1=s1, scalar2=None, op0=op0), reads, writes)
            else:
                kb.op(eng, lambda e: e.tensor_scalar(out=out, in0=in0, scalar1=s1, scalar2=s2, op0=op0, op1=op1), reads, writes)

        def stt(out, in0, scalar, in1, op0, op1, reads, writes):
            kb.op("dve", lambda e: e.scalar_tensor_tensor(out=out, in0=in0, scalar=scalar, in1=in1, op0=op0, op1=op1), reads, writes)

        def recip(out, in_, reads, writes):
            kb.op("dve", lambda e: e.reciprocal(out=out, in_=in_), reads, writes)

        def cp(eng, out, in_, reads, writes):
            if eng == "act":
                kb.op("act", lambda e: e.copy(out=out, in_=in_), reads, writes)
            else:
                kb.op(eng, lambda e: e.tensor_copy(out=out, in_=in_), reads, writes)

        def memset(eng, ap, val, writes):
            kb.op(eng, lambda e: e.memset(ap, val), (), writes)

        def store(dst, tile_ap, tilek, sem):
            kb.dma("sp", dst, tile_ap, sem, reads=[tilek])

        def wload(W2d, KC, c0, gw, k0=0):
            slot, sem = wring.next()
            v = slot[:, 0:KC * gw].rearrange("p (k n) -> p k n", k=KC)
            src = W2d.rearrange("(kc p) n -> p kc n", p=P)[:, k0:k0 + KC, c0:c0 + gw]
            kb.dma("pool", v, src, sem, writes=[slot.k])
            return slot, v

        kb.dma("sp", cst[:], cst_in[:, 0:1024], misc_sem, writes=[cst.k])
        kb.dma("sp", ccol[:], c_col, misc_sem, writes=[ccol.k])
        kb.dma("sp", gm_t[:].rearrange("p (l c) -> p l c", l=2), gmix.rearrange("l p c -> p l c"), misc_sem, writes=[gm_t.k])
        kb.dma("sp", gf_t[:].rearrange("p (l c) -> p l c", l=2), gffn.rearrange("l p c -> p l c"), misc_sem, writes=[gf_t.k])
        kb.dma("sp", adab_t[:].rearrange("p (l c) -> p l c", l=2), ada_b.rearrange("l p c -> p l c"), misc_sem, writes=[adab_t.k])
        cp("dve", ones_bf[:], ones32, [cst.k], [ones_bf.k])
        cp("dve", trige_bf[:], tri_ge, [cst.k], [trige_bf.k])
        memset("dve", oh_bf[:], 0.0, [oh_bf.k])
        for n in range(8):
            cp("dve", oh_bf[:, n * P:(n + 1) * P], ident[:, n:n + 1].to_broadcast([P, P]), [cst.k], [oh_bf.k])
        act(cact2[:, :, 0], ccol[:], AF.Silu, [ccol.k], [cact2.k])
        act(cact2[:, :, 1], ccol[:], AF.Silu, [ccol.k], [cact2.k])

        def ada_group(l, fg, ring):
            bk = banks[l]
            wv = ada_w[l].rearrange("(kc p) f -> p kc f", p=P)
            slot, sem = ring.next()
            kb.dma("sp", slot[:], wv[:, :, fg * 512:(fg + 1) * 512], sem, writes=[slot.k])
            for fl in range(4):
                col = fg * 4 + fl
                for kc in range(DC):
                    mm(bk[:, 2 * col:2 * col + 2], slot[:, kc, fl * P:(fl + 1) * P], cact2[:, kc, :],
                       kc == 0, kc == DC - 1, [slot.k, cact2.k], [bk.k])

        def ada_finish(l):
            bk = banks[l]
            tt("dve", modv[l][:], bk[:, 0:192].rearrange("p (c two) -> p c two", two=2)[:, :, 0],
               adab_t[:, l * 96:(l + 1) * 96], ALU.add, [bk.k, adab_t.k], [modv[l].k])
            m = modv[l]
            stt(A1[l][:], m[:, 16:32], 1.0, gm_t[:, l * DC:(l + 1) * DC], ALU.add, ALU.mult, [m.k, gm_t.k], [A1[l].k])
            ts("dve", G1[l][:], m[:, 32:48], 1.0, None, ALU.add, None, [m.k], [G1[l].k])
            stt(A2[l][:], m[:, 64:80], 1.0, gf_t[:, l * DC:(l + 1) * DC], ALU.add, ALU.mult, [m.k, gf_t.k], [A2[l].k])
            ts("dve", G2[l][:], m[:, 80:96], 1.0, None, ALU.add, None, [m.k], [G2[l].k])
            if "MODD" in dbg:
                store(MODD[l], m[:], m.k, misc_sem)

        with ExitStack() as ph:
            psb = mk_alloc(ph)
            aring = Ring(kb, psb, 2, [P, DC, 512], F32, "ada", sems=dsems[0:2])
            for fg in range(24 if "P0" not in skip else 1):
                ada_group(0, fg, aring)
            ada_finish(0)
            kb.barrier()
        B1 = [modv[l][:, 0:16] for l in range(2)]
        B2 = [modv[l][:, 48:64] for l in range(2)]

        def norm_chunk(sb_tmp, Xc, Acol, Bcol, modk, out_fn, outk):
            sq, rs, rstd, tmps = sb_tmp
            bk = nbank()
            for dc in range(DC):
                sq1 = sq[dc % 2]
                act(sq1[:], Xc[:, dc, :], AF.Square, [Xc.k], [sq1.k])
                mm(bk[:], ones_bf[:], sq1[:], dc == 0, dc == DC - 1, [ones_bf.k, sq1.k], [bk.k])
            act(rs[:], bk[:], AF.Sqrt, [bk.k], [rs.k], bias=1e-6, scale=1.0 / D)
            recip(rstd[:], rs[:], [rs.k], [rstd.k])
            for dc in range(DC):
                tmp = tmps[dc % 2]
                stt(tmp[:], Xc[:, dc, :], Acol[:, dc:dc + 1], rstd[:], ALU.mult, ALU.mult,
                    [Xc.k, modk[0], rstd.k], [tmp.k])
                act(out_fn(dc), tmp[:], AF.Identity, [tmp.k, modk[1]], [outk], bias=Bcol[:, dc:dc + 1], scale=1.0)

        def norm_tmp(psb, tag):
            return ([Tile(psb([P, 512], BF16, f"sq{tag}{i}")) for i in range(2)], Tile(psb([P, 512], F32, "rs" + tag)),
                    Tile(psb([P, 512], F32, "rstd" + tag)),
                    [Tile(psb([P, 512], F32, f"ntmp{tag}{i}")) for i in range(2)])

        def phase_norm_full(l, src, HM):
            with ExitStack() as ph:
                psb = mk_alloc(ph)
                xr = Ring(kb, psb, 1, [P, DC, 512], F32, "xr", sems=dsems[0:1])
                ntmp = norm_tmp(psb, "a")
                sv = src.rearrange("(c p) t -> p c t", p=P)
                for tc in range(4):
                    Xc, sem = xr.next()
                    kb.dma("sp", Xc[:], sv[:, :, tc * 512:(tc + 1) * 512], sem, writes=[Xc.k])
                    norm_chunk(ntmp, Xc, A1[l], B1[l], (A1[l].k, modv[l].k),
                               lambda dc: HM[:, dc, tc * 512:(tc + 1) * 512], HM.k)
                kb.barrier()

        def proj_fm(W2d, c0, ncols, HM, evac):
            if "PROJ" in skip:
                return
            for g0 in range(0, ncols, 512):
                slot, wv = wload(W2d, DC, c0 + g0, 512)
                for ocl in range(4):
                    for tc in range(4):
                        bk = nbank()
                        for kc in range(DC):
                            mm(bk[:], wv[:, kc, ocl * P:(ocl + 1) * P], HM[:, kc, tc * 512:(tc + 1) * 512],
                               kc == 0, kc == DC - 1, [slot.k, HM.k], [bk.k])
                        evac(g0 // P + ocl, tc, bk)

        def proj_tm(W2d, c0, ncols, HM, evac):
            if "PROJ" in skip:
                return
            for g0 in range(0, ncols, 512):
                gw = min(512, ncols - g0)
                slot, wv = wload(W2d, DC, c0 + g0, gw)
                for t16 in range(16):
                    bk = nbank()
                    for kc in range(DC):
                        mm(bk[:, 0:gw], HM[:, kc, t16 * P:(t16 + 1) * P], wv[:, kc, :],
                           kc == 0, kc == DC - 1, [slot.k, HM.k], [bk.k])
                    evac(t16, g0, gw, bk)

        def phase_out_ffn(l, xsrc, KCm, w_out2d, mixrows, xdst_mid, xdst_final):
            with ExitStack() as outer:
                osb = mk_alloc(outer)
                HF = Tile(osb([P, DC, S], BF16, "HF"))
                with ExitStack() as ph:
                    psb = mk_alloc(ph)
                    Xt = Tile(psb([P, DC, 512], F32, "Xt"))
                    Mt = Tile(psb([P, KCm, 512], BF16, "Mt"))
                    ntmp = norm_tmp(psb, "f")
                    gw = 256 if KCm == 32 else 512
                    xv = xsrc.rearrange("(c p) t -> p c t", p=P)
                    mv = MIX[0:mixrows, :].rearrange("(c p) t -> p c t", p=P)
                    x1v = xdst_mid.rearrange("(c p) t -> p c t", p=P)
                    for tc in range(4):
                        tsl = slice(tc * 512, (tc + 1) * 512)
                        kb.dma("sp", Xt[:], xv[:, :, tsl], dsems[0], writes=[Xt.k])
                        kb.dma("sp", Mt[:], mv[:, :, tsl], dsems[1], writes=[Mt.k])
                        for g0 in range(0, D, gw):
                            slot, wv = wload(w_out2d, KCm, g0, gw)
                            for ocl in range(gw // P):
                                dc = g0 // P + ocl
                                bk = nbank()
                                for kc in range(KCm):
                                    mm(bk[:], wv[:, kc, ocl * P:(ocl + 1) * P], Mt[:, kc, :],
                                       kc == 0, kc == KCm - 1, [slot.k, Mt.k], [bk.k])
                                stt(Xt[:, dc, :], bk[:], G1[l][:, dc:dc + 1], Xt[:, dc, :], ALU.mult, ALU.add,
                                    [bk.k, G1[l].k, Xt.k], [Xt.k])
                        kb.dma("sp", x1v[:, :, tsl], Xt[:], dsems[2], reads=[Xt.k])
                        norm_chunk(ntmp, Xt, A2[l], B2[l], (A2[l].k, modv[l].k),
                                   lambda dc: HF[:, dc, tsl], HF.k)
                    kb.barrier()
                if stop_after == f"F{l}":
                    return
                with ExitStack() as ph:
                    psb = mk_alloc(ph)
                    sgs = [Tile(psb([P, 512], F32, f"sg{i}")) for i in range(2)]
                    cnt = 0
                    for fg in range(11):
                        sg_slot, wg = wload(w_gate[l], DC, fg * 512, 512)
                        su_slot, wu = wload(w_up[l], DC, fg * 512, 512)
                        for fl in range(4):
                            fc = fg * 4 + fl
                            for tc in range(4):
                                tsl = slice(tc * 512, (tc + 1) * 512)
                                bg = nbank()
                                bu = nbank()
                                for kc in range(DC):
                                    mm(bg[:], wg[:, kc, fl * P:(fl + 1) * P], HF[:, kc, tsl], kc == 0, kc == DC - 1,
                                       [sg_slot.k, HF.k], [bg.k])
                                for kc in range(DC):
                                    mm(bu[:], wu[:, kc, fl * P:(fl + 1) * P], HF[:, kc, tsl], kc == 0, kc == DC - 1,
                                       [su_slot.k, HF.k], [bu.k])
                                sg = sgs[cnt % 2]
                                cnt += 1
                                act(sg[:], bg[:], AF.Silu, [bg.k], [sg.k])
                                so, ssem = stgb.next()
                                tt("dve", so[:], sg[:], bu[:], ALU.mult, [sg.k, bu.k], [so.k])
                                store(AT[fc * P:(fc + 1) * P, tsl], so[:], so.k, ssem)
                    kb.barrier()
            if stop_after == f"G{l}":
                return
            with ExitStack() as ph:
                psb = mk_alloc(ph)
                At = Tile(psb([P, FC, 1024], BF16, "At"))
                xgr = Ring(kb, psb, 2, [P, 2, 1024], F32, "xgr", sems=dsems[2:4])
                x1v = xdst_mid.rearrange("(c p) t -> p c t", p=P)
                av = AT.rearrange("(c p) t -> p c t", p=P)
                ov = xdst_final.rearrange("(c p) t -> p c t", p=P)
                for tp2 in range(2):
                    tsl = slice(tp2 * 1024, (tp2 + 1) * 1024)
                    kb.dma("sp", At[:], av[:, :, tsl], dsems[1], writes=[At.k])
                    for g0 in range(0, D, 256):
                        dc0 = g0 // P
                        Xg, xsem = xgr.next()
                        kb.dma("sp", Xg[:], x1v[:, dc0:dc0 + 2, tsl], xsem, writes=[Xg.k])
                        bks = [[nbank(), nbank()], [nbank(), nbank()]]
                        for half in range(2):
                            slot, wv = wload(w_down[l], 22, g0, 256, k0=22 * half)
                            for ocl in range(2):
                                for tl in range(2):
                                    bk = bks[ocl][tl]
                                    for kc in range(22):
                                        mm(bk[:], wv[:, kc, ocl * P:(ocl + 1) * P],
                                           At[:, 22 * half + kc, tl * 512:(tl + 1) * 512],
                                           half == 0 and kc == 0, half == 1 and kc == 21, [slot.k, At.k], [bk.k])
                        for ocl in range(2):
                            dc = dc0 + ocl
                            for tl in range(2):
                                bk = bks[ocl][tl]
                                xs_ = Xg[:, ocl, tl * 512:(tl + 1) * 512]
                                stt(xs_, bk[:], G2[l][:, dc:dc + 1], xs_, ALU.mult, ALU.add,
                                    [bk.k, G2[l].k, Xg.k], [Xg.k])
                        kb.dma("sp", ov[:, dc0:dc0 + 2, tsl], Xg[:], xsem, reads=[Xg.k])
                kb.barrier()

        def layer0():
            with ExitStack() as outer:
                osb = mk_alloc(outer)
                HM = Tile(osb([P, DC, S], BF16, "HM"))
                if "A0" not in skip:
                    phase_norm_full(0, xT, HM)
                if stop_after == "A0":
                    return
                with ExitStack() as ph:
                    if "B0" in skip:
                        raise_skip = True
                    else:
                        raise_skip = False
                    psb = mk_alloc(ph)
                    dtb = Tile(psb([P, 32], F32, "dtb"))
                    dtt = [Tile(psb([P, 32], F32, f"dtt{i}")) for i in range(2)]
                    kb.dma("sp", dtb[:], dtb_in, misc_sem, writes=[dtb.k])

                    def ev_u(oc, tc, bk):
                        so, ssem = stgb.next()
                        act(so[:], bk[:], AF.Gelu_apprx_tanh, [bk.k], [so.k])
                        store(UT[oc * P:(oc + 1) * P, tc * 512:(tc + 1) * 512], so[:], so.k, ssem)
                    proj_fm(ab_w_in, 0, 2048, HM, ev_u)

                    def ev_v(t16, g0, gw, bk):
                        so, ssem = stg.next()
                        act(so[:], bk[:], AF.Gelu_apprx_tanh, [bk.k], [so.k])
                        store(VTOK[t16 * P:(t16 + 1) * P, g0:g0 + gw], so[:], so.k, ssem)
                    proj_tm(ab_w_in, 2048, 2048, HM, ev_v)

                    def ev_z(t16, g0, gw, bk):
                        so, ssem = stg.next()
                        act(so[:], bk[:], AF.Silu, [bk.k], [so.k])
                        store(SZ[t16 * P:(t16 + 1) * P, g0:g0 + gw], so[:], so.k, ssem)
                    proj_tm(ab_w_in, 4096, 2048, HM, ev_z)

                    ecnt = [0]

                    def ev_x(oc, tc, bk):
                        so, ssem = stg.next()
                        cp("act" if ecnt[0] % 2 else "dve", so[:], bk[:], [bk.k], [so.k])
                        ecnt[0] += 1
                        store(XBCT[oc * P:(oc + 1) * P, tc * 512:(tc + 1) * 512], so[:], so.k, ssem)
                    proj_fm(ab_w_in, 6144, 3072, HM, ev_x)

                    def ev_dt(t16, g0, gw, bk):
                        d1 = dtt[t16 % 2]
                        so, ssem = stg.next()
                        tt("dve", d1[:], bk[:, 0:32], dtb[:], ALU.add, [bk.k, dtb.k], [d1.k])
                        act(so[:, 0:32], d1[:], AF.Softplus, [d1.k], [so.k])
                        store(DTs[t16 * P:(t16 + 1) * P, :], so[:, 0:32], so.k, ssem)
                    proj_tm(ab_w_in, 9216, 32, HM, ev_dt)
                    kb.barrier()
            if stop_after == "B0":
                return
            with ExitStack() as ph:
                psb = mk_alloc(ph)
                wsT32 = Tile(psb([P, DC, P], F32, "wsT32"))
                wsT = Tile(psb([P, DC, P], BF16, "wsT"))
                bsbc = Tile(psb([P, DC, P], F32, "bsbc"))
                T2 = Tile(psb([P, DC, P], F32, "T2"))
                lng_t = Tile(psb([P, DC], F32, "lng_t"))
                lnb_t = Tile(psb([P, DC], F32, "lnb_t"))
                kb.dma("sp", wsT32[:], wsT_in.rearrange("p (h t) -> p h t", h=DC), misc_sem, writes=[wsT32.k])
                kb.dma("sp", bsbc[:], bsbc_in.rearrange("p (h t) -> p h t", h=DC), misc_sem, writes=[bsbc.k])
                kb.dma("sp", lng_t[:], lng, misc_sem, writes=[lng_t.k])
                kb.dma("sp", lnb_t[:], lnb, misc_sem, writes=[lnb_t.k])
                tt("dve", wsT[:], wsT32[:], tri_le.unsqueeze(1).to_broadcast([P, DC, P]), ALU.mult,
                   [wsT32.k, cst.k], [wsT.k])
                for q in range(4):
                    bk = nbank()
                    mm(bk[:], ones_bf[:], wsT[:, 4 * q:4 * q + 4, :].rearrange("p h t -> p (h t)"), True, True,
                       [ones_bf.k, wsT.k], [bk.k])
                    for hl in range(4):
                        h = 4 * q + hl
                        stt(T2[:, h, :], bk[:, hl * P:(hl + 1) * P], lnb_t[:, h:h + 1], bsbc[:, h, :], ALU.mult, ALU.add,
                            [bk.k, lnb_t.k, bsbc.k], [T2.k])
                vr = Ring(kb, psb, 2, [P, D], F32, "vr", sems=dsems[0:2])
                ur = Ring(kb, psb, 2, [P, DC, P], BF16, "ur", sems=dsems[2:4])
                yr = Ring(kb, psb, 2, [P, DC, P], BF16, "yr", sems=dsems[4:6])
                nb = Tile(psb([P, D], BF16, "nb"))
                st6 = Tile(psb([P, 4, 6], F32, "st6"))
                mvv = Tile(psb([P, 2], F32, "mvv"))
                sd = Tile(psb([P, 1], F32, "sd"))
                rr = Tile(psb([P, 1], F32, "rr"))
                tmpa = [Tile(psb([P, 4, P], F32, f"tmpa{i}")) for i in range(2)]
                utv = UT.rearrange("(h p) t -> p h t", p=P)
                mxv = MIX[0:D, :].rearrange("(h p) t -> p h t", p=P)
                for t16 in range(16 if "C0" not in skip else 0):
                    tsl = slice(t16 * P, (t16 + 1) * P)
                    Vt, vsem = vr.next()
                    Ut, usem = ur.next()
                    Ya, ysem = yr.next()
                    kb.dma("sp", Vt[:], VTOK[tsl, :], vsem, writes=[Vt.k])
                    kb.dma("sp", Ut[:], utv[:, :, tsl], usem, writes=[Ut.k])
                    for j in range(4):
                        kb.op("dve", lambda e, j=j: e.bn_stats(out=st6[:, j, :], in_=Vt[:, j * 512:(j + 1) * 512]),
                              [Vt.k], [st6.k])
                    kb.op("dve", lambda e: e.bn_aggr(out=mvv[:], in_=st6[:].rearrange("p a b -> p (a b)")), [st6.k], [mvv.k])
                    act(sd[:], mvv[:, 1:2], AF.Sqrt, [mvv.k], [sd.k], bias=1e-5, scale=1.0)
                    recip(rr[:], sd[:], [sd.k], [rr.k])
                    ts("dve", nb[:], Vt[:], mvv[:, 0:1], rr[:], ALU.subtract, ALU.mult, [Vt.k, mvv.k, rr.k], [nb.k])
                    for q in range(4):
                        bk = nbank()
                        for hl in range(4):
                            h = 4 * q + hl
                            mm(bk[:, hl * P:(hl + 1) * P], nb[:, h * P:(h + 1) * P], wsT[:, h, :], True, True,
                               [nb.k, wsT.k], [bk.k])
                        tm = tmpa[q % 2]
                        for hl in range(4):
                            h = 4 * q + hl
                            stt(tm[:, hl, :], bk[:, hl * P:(hl + 1) * P], lng_t[:, h:h + 1], T2[:, h, :], ALU.mult, ALU.add,
                                [bk.k, lng_t.k, T2.k], [tm.k])
                        tt("dve", Ya[:, 4 * q:4 * q + 4, :], tm[:], Ut[:, 4 * q:4 * q + 4, :], ALU.mult,
                           [tm.k, Ut.k], [Ya.k])
                    kb.dma("sp", mxv[:, :, tsl], Ya[:], ysem, reads=[Ya.k])
                kb.barrier()
            if stop_after == "C0":
                return
            with ExitStack() as ph:
                psb = mk_alloc(ph)
                cw = Tile(psb([P, 24, 4], F32, "cw"))
                cb = Tile(psb([P, 24], F32, "cb"))
                kb.dma("sp", cw[:], cw_in.rearrange("p (c k) -> p c k", k=4), misc_sem, writes=[cw.k])
                kb.dma("sp", cb[:], cb_in, misc_sem, writes=[cb.k])
                xrr = Ring(kb, psb, 2, [P, S + 4], F32, "xrr", sems=dsems[0:2])
                accs = [Tile(psb([P, S], F32, f"acc{i}")) for i in range(2)]
                xo = Ring(kb, psb, 2, [P, S], F32, "xo", sems=dsems[2:4])
                aring1 = Ring(kb, psb, 2, [P, DC, 512], F32, "ada1", sems=dsems[4:6])
                for sl_ in xrr.slots:
                    memset("dve", sl_[:, 0:4], 0.0, [sl_.k])
                for cc in range(24 if "D0" not in skip else 0):
                    ada_group(1, cc, aring1)
                    XR, xsem = xrr.next()
                    acc = accs[cc % 2]
                    XO, osem = xo.next()
                    kb.dma("sp", XR[:, 4:S + 4], XBCT[cc * P:(cc + 1) * P, :], xsem, writes=[XR.k])
                    ts("dve", acc[:], XR[:, 4:S + 4], cw[:, cc, 3:4], cb[:, cc:cc + 1], ALU.mult, ALU.add,
                       [XR.k, cw.k, cb.k], [acc.k])
                    for k in (2, 1, 0):
                        stt(acc[:], XR[:, 1 + k:S + 1 + k], cw[:, cc, k:k + 1], acc[:], ALU.mult, ALU.add,
                            [XR.k, cw.k, acc.k], [acc.k])
                    act(XO[:], acc[:], AF.Silu, [acc.k], [XO.k])
                    kb.dma("sp", XCT[cc * P:(cc + 1) * P, :], XO[:], osem, reads=[XO.k])
                ada_finish(1)
                kb.barrier()
            if stop_after == "D0":
                return
            with ExitStack() as ph:
                psb = mk_alloc(ph)
                S32 = Tile(psb([P, D], F32, "S32"))
                Sbf = Tile(psb([P, D], BF16, "Sbf"))
                abc = Tile(psb([P, 32], F32, "abc"))
                alog = Tile(psb([P, 32], F32, "alog"))
                Dbc = Tile(psb([P, D], F32, "Dbc"))
                gbc = Tile(psb([P, D], F32, "gbc"))
                kb.dma("sp", alog[:], alog_in, misc_sem, writes=[alog.k])
                kb.dma("sp", Dbc[:], dbc_in, misc_sem, writes=[Dbc.k])
                kb.dma("sp", gbc[:], sgbc_in, misc_sem, writes=[gbc.k])
                act(abc[:], alog[:], AF.Exp, [alog.k], [abc.k])
                ts("dve", abc[:], abc[:], -1.0, None, ALU.mult, None, [abc.k], [abc.k])
                memset("dve", S32[:], 0.0, [S32.k])
                memset("dve", Sbf[:], 0.0, [Sbf.k])
                xcr = Ring(kb, psb, 2, [P, 24, P], F32, "xcr", sems=dsems[0:2])
                szr = Ring(kb, psb, 2, [P, D], F32, "szr", sems=dsems[2:4])
                dtr = Ring(kb, psb, 2, [P, 32], F32, "dtr", sems=dsems[4:6])
                ybr = Ring(kb, psb, 2, [P, DC, P], BF16, "ybr", sems=dsems[6:8])
                adt = Tile(psb([P, 32], F32, "adt"))
                eac = Tile(psb([P, 64], F32, "eac"))
                xs_tok = Tile(psb([P, D], F32, "xs_tok"))
                xdt = Tile(psb([P, D], BF16, "xdt"))
                xdte = Tile(psb([P, D], BF16, "xdte"))
                Btok = Tile(psb([P, 4, P], BF16, "Btok"))
                BCT = Tile(psb([P, 8, P], BF16, "BCT"))
                mCB = Tile(psb([P, 4, P], F32, "mCB"))
                A4s = [Tile(psb([P, 4, P], F32, f"A4{i}")) for i in range(2)]
                E4s = [Tile(psb([P, 4, P], F32, f"E4{i}")) for i in range(2)]
                MTall = Tile(psb([P, 32, P], BF16, "MTall"))
                dte = Tile(psb([P, 32], F32, "dte"))
                t1 = Tile(psb([P, 512], F32, "t1"))
                t3 = Tile(psb([P, 512], F32, "t3"))
                yg = Tile(psb([P, D], F32, "yg"))
                ss = Tile(psb([P, 1], F32, "ss"))
                sd = Tile(psb([P, 1], F32, "sd2"))
                rr = Tile(psb([P, 1], F32, "rr2"))
                yn = Tile(psb([P, D], F32, "yn"))
                tS = Tile(psb([P, 512], F32, "tS"))
                xcv = XCT.rearrange("(c p) t -> p c t", p=P)
                mxv = MIX[D:2 * D, :].rearrange("(h p) t -> p h t", p=P)
                for c in range(nchunk):
                    tsl = slice(c * P, (c + 1) * P)
                    XC, s0 = xcr.next()
                    SZc, s1 = szr.next()
                    DTc, s2 = dtr.next()
                    YB, s3 = ybr.next()
                    kb.dma("sp", XC[:], xcv[:, :, tsl], s0, writes=[XC.k])
                    kb.dma("sp", SZc[:], SZ[tsl, :], s1, writes=[SZc.k])
                    kb.dma("sp", DTc[:], DTs[tsl, :], s2, writes=[DTc.k])
                    tt("dve", adt[:], DTc[:], abc[:], ALU.mult, [DTc.k, abc.k], [adt.k])
                    bA = nbank()
                    mm(bA[:, 0:32], tri_le, adt[:], True, True, [cst.k, adt.k], [bA.k])
                    mm(bA[:, 32:64], ones32, adt[:], True, True, [cst.k, adt.k], [bA.k])
                    act(eac[:], bA[:, 0:64], AF.Exp, [bA.k], [eac.k])
                    if e0step <= 1:
                        continue
                    for q in range(4):
                        bk = nbank()
                        for i in range(4):
                            tp(bk[:, i * P:(i + 1) * P], XC[:, 4 * q + i, :], [XC.k], [bk.k])
                        cp("act", xs_tok[:, q * 512:(q + 1) * 512], bk[:], [bk.k], [xs_tok.k])
                        if e0step <= 1.2:
                            continue
                        tt("dve", xdt[:, q * 512:(q + 1) * 512].rearrange("p (h j) -> p j h", j=64),
                           xs_tok[:, q * 512:(q + 1) * 512].rearrange("p (h j) -> p j h", j=64),
                           DTc[:, 8 * q:8 * q + 8].unsqueeze(1).to_broadcast([P, 64, 8]), ALU.mult,
                           [xs_tok.k, DTc.k], [xdt.k])
                    if e0step <= 1.4:
                        continue
                    bB = nbank()
                    for g in range(4):
                        tp(bB[:, g * P:(g + 1) * P], XC[:, 16 + g, :], [XC.k], [bB.k])
                    cp("act", Btok[:].rearrange("p g n -> p (g n)"), bB[:], [bB.k], [Btok.k])
                    if e0step <= 1.6:
                        continue
                    cp("pool", BCT[:], XC[:, 16:24, :], [XC.k], [BCT.k])
                    if e0step <= 2:
                        continue
                    bC = nbank()
                    for g in range(4):
                        mm(bC[:, g * P:(g + 1) * P], BCT[:, g, :], BCT[:, 4 + g, :], True, True, [BCT.k], [bC.k])
                    tt("dve", mCB[:], bC[:].rearrange("p (g t) -> p g t", g=4),
                       tri_le.unsqueeze(1).to_broadcast([P, 4, P]), ALU.mult, [bC.k, cst.k], [mCB.k])
                    if e0step <= 3:
                        continue
                    for g in range(4):
                        for half in range(2):
                            b8 = 2 * g + half
                            A4 = A4s[b8 % 2]
                            E4 = E4s[b8 % 2]
                            for hl in range(4):
                                ts("dve", A4[:, hl, :], sl_gt, adt[:, 4 * b8 + hl:4 * b8 + hl + 1], None, ALU.mult, None,
                                   [cst.k, adt.k], [A4.k])
                            bk = nbank()
                            for hl in range(4):
                                mm(bk[:, hl * P:(hl + 1) * P], A4[:, hl, :], tri_le, True, True, [A4.k, cst.k], [bk.k])
                            act(E4[:].rearrange("p h t -> p (h t)"), bk[:], AF.Exp, [bk.k], [E4.k])
                            tt("dve", MTall[:, 4 * b8:4 * b8 + 4, :], E4[:],
                               mCB[:, g:g + 1, :].to_broadcast([P, 4, P]), ALU.mult, [E4.k, mCB.k], [MTall.k])
                            cp("act", dte[:, 4 * b8:4 * b8 + 4], E4[:, :, P - 1], [E4.k], [dte.k])
                    for g in range(4):
                        bY = nbank()
                        for hh in range(8):
                            h = 8 * g + hh
                            mm(bY[:, hh * 64:(hh + 1) * 64], MTall[:, h, :], xdt[:, h * 64:(h + 1) * 64], True, True,
                               [MTall.k, xdt.k], [bY.k])
                        bO = nbank()
                        mm(bO[:], BCT[:, 4 + g, :], Sbf[:, g * 512:(g + 1) * 512], True, True, [BCT.k, Sbf.k], [bO.k])
                        gs = slice(g * 512, (g + 1) * 512)
                        tt("dve", t1[:].rearrange("p (h j) -> p j h", j=64), bO[:].rearrange("p (h j) -> p j h", j=64),
                           eac[:, 8 * g:8 * g + 8].unsqueeze(1).to_broadcast([P, 64, 8]), ALU.mult,
                           [bO.k, eac.k], [t1.k])
                        tt("pool", t3[:], xs_tok[:, gs], Dbc[:, gs], ALU.mult, [xs_tok.k, Dbc.k], [t3.k])
                        tt("dve", t1[:], t1[:], bY[:], ALU.add, [t1.k, bY.k], [t1.k])
                        tt("dve", t1[:], t1[:], t3[:], ALU.add, [t1.k, t3.k], [t1.k])
                        tt("dve", yg[:, gs], t1[:], SZc[:, gs], ALU.mult, [t1.k, SZc.k], [yg.k])
                    if e0step <= 4:
                        continue
                    tt("dve", xdte[:].rearrange("p (h j) -> p j h", j=64), xdt[:].rearrange("p (h j) -> p j h", j=64),
                       dte[:].unsqueeze(1).to_broadcast([P, 64, 32]), ALU.mult, [xdt.k, dte.k], [xdte.k])
                    for g in range(4):
                        gs = slice(g * 512, (g + 1) * 512)
                        bS = nbank()
                        mm(bS[:], Btok[:, g, :], xdte[:, gs], True, True, [Btok.k, xdte.k], [bS.k])
                        tt("dve", tS[:].rearrange("p (h j) -> p j h", j=64), S32[:, gs].rearrange("p (h j) -> p j h", j=64),
                           eac[:, 32 + 8 * g:32 + 8 * g + 8].unsqueeze(1).to_broadcast([P, 64, 8]), ALU.mult,
                           [S32.k, eac.k], [tS.k])
                        tt("dve", S32[:, gs], tS[:], bS[:], ALU.add, [tS.k, bS.k], [S32.k])
                        cp("pool", Sbf[:, gs], S32[:, gs], [S32.k], [Sbf.k])
                    if e0step <= 5:
                        continue
                    act(xdte[:], yg[:], AF.Square, [yg.k], [xdte.k, ss.k], accum=ss[:])
                    act(sd[:], ss[:], AF.Sqrt, [ss.k], [sd.k], bias=1e-6, scale=1.0 / D)
                    recip(rr[:], sd[:], [sd.k], [rr.k])
                    stt(yn[:], yg[:], rr[:], gbc[:], ALU.mult, ALU.mult, [yg.k, rr.k, gbc.k], [yn.k])
                    for q in range(4):
                        bk = nbank()
                        for i in range(4):
                            tp(bk[:, i * P:(i + 1) * P], yn[:, (4 * q + i) * P:(4 * q + i + 1) * P], [yn.k], [bk.k])
                        cp("act", YB[:, 4 * q:4 * q + 4, :].rearrange("p h t -> p (h t)"), bk[:], [bk.k], [YB.k])
                    kb.dma("sp", mxv[:, :, tsl], YB[:], s3, reads=[YB.k])
                kb.barrier()
            if stop_after == "E0":
                return
            phase_out_ffn(0, xT, 32, ab_w_out, 2 * D, X1, X2)

        def attn_loads(psb, qrow, vcol, tag):
            QTh = Tile(psb([P, S], BF16, "QTh" + tag))
            KTh = Tile(psb([P, S], BF16, "KTh" + tag))
            Vh = Tile(psb([P, 16, P], BF16, "Vh" + tag))
            return QTh, KTh, Vh

        def layer1():
            with ExitStack() as outer:
                osb = mk_alloc(outer)
                HM = Tile(osb([P, DC, S], BF16, "HM1"))
                phase_norm_full(1, X2, HM)
                if stop_after == "A1":
                    return
                with ExitStack() as ph:
                    psb = mk_alloc(ph)
                    gq = Tile(psb([P, 1], F32, "gq"))
                    gk = Tile(psb([P, 1], F32, "gk"))
                    kb.dma("sp", gq[:], gq_col, misc_sem, writes=[gq.k])
                    kb.dma("sp", gk[:], gk_col, misc_sem, writes=[gk.k])
                    sqb = [Tile(psb([P, 512], BF16, f"sqb{i}")) for i in range(2)]
                    rsb = [Tile(psb([P, 512], F32, f"rsb{i}")) for i in range(2)]
                    ecnt = [0]

                    def mk_ev_copy(dst, row0):
                        def ev(oc, tc, bk):
                            so, ssem = stgb.next()
                            cp("act" if ecnt[0] % 2 else "dve", so[:], bk[:], [bk.k], [so.k])
                            ecnt[0] += 1
                            store(dst[row0 + oc * P:row0 + (oc + 1) * P, tc * 512:(tc + 1) * 512], so[:], so.k, ssem)
                        return ev

                    def mk_ev_tok(col0):
                        def ev(t16, g0, gw, bk):
                            so, ssem = stgb.next()
                            cp("act" if ecnt[0] % 2 else "dve", so[:], bk[:], [bk.k], [so.k])
                            ecnt[0] += 1
                            store(VT[t16 * P:(t16 + 1) * P, col0 + g0:col0 + g0 + gw], so[:], so.k, ssem)
                        return ev

                    def mk_ev_norm(dst, row0, gcol):
                        def ev(oc, tc, bk):
                            i = ecnt[0] % 2
                            ecnt[0] += 1
                            sq, rs = sqb[i], rsb[i]
                            act(sq[:], bk[:], AF.Square, [bk.k], [sq.k])
                            b2 = nbank()
                            mm(b2[:], ones_bf[:], sq[:], True, True, [ones_bf.k, sq.k], [b2.k])
                            act(rs[:], b2[:], AF.Sqrt, [b2.k], [rs.k], bias=1e-6, scale=1.0 / P)
                            recip(rs[:], rs[:], [rs.k], [rs.k])
                            so, ssem = stgb.next()
                            stt(so[:], bk[:], gcol[:, 0:1], rs[:], ALU.mult, ALU.mult, [bk.k, gcol.k, rs.k], [so.k])
                            store(dst[row0 + oc * P:row0 + (oc + 1) * P, tc * 512:(tc + 1) * 512], so[:], so.k, ssem)
                        return ev

                    proj_fm(cd_w_in, 0, 1536, HM, mk_ev_copy(QT, 0))
                    proj_fm(cd_w_in, 1536, 1536, HM, mk_ev_copy(KT, 0))
                    proj_tm(cd_w_in, 3072, 1536, HM, mk_ev_tok(0))
                    proj_fm(cd_w_in, 4608, 512, HM, mk_ev_norm(QT, 1536, gq))
                    proj_fm(cd_w_in, 5120, 512, HM, mk_ev_norm(KT, 1536, gk))
                    proj_tm(cd_w_in, 5632, 512, HM, mk_ev_tok(1536))
                    kb.barrier()
            if stop_after == "B1":
                return
            sc = 1.0 / math.sqrt(128.0)
            with ExitStack() as ph:
                psb = mk_alloc(ph)
                qr = Ring(kb, psb, 4, [P, S], BF16, "qr", sems=dsems[0:4])
                kr = Ring(kb, psb, 4, [P, S], BF16, "kr", sems=dsems[4:8])
                vr = Ring(kb, psb, 4, [P, 16, P], BF16, "vr1", sems=dsems[8:12])
                mstr_bf = Tile(psb([P, 4, 512], BF16, "mstr_bf"))
                m32 = Tile(psb([P, 4, 512], F32, "m32"))
                kb.dma("sp", m32[:], cst_in[:, MS0:MS0 + 2048].rearrange("p (r t) -> p r t", r=4), misc_sem, writes=[m32.k])
                cp("dve", mstr_bf[:], m32[:], [m32.k], [mstr_bf.k])
                NB = 4
                e32s = [[Tile(psb([P, 512], F32, f"e32{s_}{i}")) for i in range(NB)] for s_ in range(2)]
                spbs = [[Tile(psb([P, 512], BF16, f"spb{s_}{i}")) for i in range(NB)] for s_ in range(2)]
                x32s = [[Tile(psb([P, 512], F32, f"x32{s_}{i}")) for i in range(NB)] for s_ in range(2)]
                wbs = [[Tile(psb([P, 512], BF16, f"wb{s_}{i}")) for i in range(NB)] for s_ in range(2)]
                spacc = [[Tile(psb([P, 512], BF16, f"spacc{s_}{i}")) for i in range(2)] for s_ in range(2)]
                vtv = VT.rearrange("(kt k) e -> k kt e", k=P)
                for hp in range(6):
                    hd2 = []
                    for s_ in range(2):
                        h = 2 * hp + s_
                        QTh, s0 = qr.next()
                        KTh, s1 = kr.next()
                        Vh, s2 = vr.next()
                        kb.dma("sp", QTh[:], QT[h * P:(h + 1) * P, :], s0, writes=[QTh.k])
                        kb.dma("sp", KTh[:], KT[h * P:(h + 1) * P, :], s1, writes=[KTh.k])
                        kb.dma("sp", Vh[:], vtv[:, :, h * P:(h + 1) * P], s2, writes=[Vh.k])
                        hd2.append((h, QTh, KTh, Vh))
                    blocks = [(q, j) for q in range(4) for j in range(4 * q + 3, -1, -1)]
                    bCs = {}

                    def stageA(n, s_):
                        q, j = blocks[n]
                        h, QTh, KTh, Vh = hd2[s_]
                        r = j - 4 * q
                        e32, spb = e32s[s_][n % NB], spbs[s_][n % NB]
                        bZ = nbank6()
                        mm(bZ[:], KTh[:, j * P:(j + 1) * P], QTh[:, q * 512:(q + 1) * 512], True, True, [KTh.k, QTh.k], [bZ.k])
                        act(e32[:], bZ[:], AF.Exp, [bZ.k], [e32.k], scale=sc)
                        act(spb[:], e32[:], AF.Ln, [e32.k], [spb.k], bias=1.0, scale=1.0)
                        if r >= 0:
                            tt("pool", spb[:], spb[:], mstr_bf[:, r, :], ALU.mult, [spb.k, mstr_bf.k], [spb.k])

                    def stageB1(n, s_):
                        q, j = blocks[n]
                        jmax = 4 * q + 3
                        spb, x32 = spbs[s_][n % NB], x32s[s_][n % NB]
                        acc_prev = spacc[s_][(jmax - j) % 2]
                        acc_next = spacc[s_][(jmax - j + 1) % 2]
                        bC = nbank6()
                        if j == jmax:
                            mm(bC[:], trige_bf[:], spb[:], True, True, [trige_bf.k, spb.k], [bC.k])
                        else:
                            mm(bC[:], trige_bf[:], spb[:], True, False, [trige_bf.k, spb.k], [bC.k])
                            mm(bC[:], ones_bf[:], acc_prev[:], False, True, [ones_bf.k, acc_prev.k], [bC.k])
                        act(x32[:], bC[:], AF.Exp, [bC.k], [x32.k], scale=-1.0)
                        if j > 0:
                            if j == jmax:
                                cp("dve", acc_next[:], spb[:], [spb.k], [acc_next.k])
                            else:
                                tt("dve", acc_next[:], acc_prev[:], spb[:], ALU.add, [acc_prev.k, spb.k], [acc_next.k])

                    def stageB2(n, s_):
                        q, j = blocks[n]
                        jmax = 4 * q + 3
                        h, QTh, KTh, Vh = hd2[s_]
                        r = j - 4 * q
                        e32, x32, wb = e32s[s_][n % NB], x32s[s_][n % NB], wbs[s_][n % NB]
                        bO = banks[s_]
                        tt("pool", wb[:], e32[:], x32[:], ALU.mult, [e32.k, x32.k], [wb.k])
                        if r >= 0:
                            tt("pool", wb[:], wb[:], mstr_bf[:, r, :], ALU.mult, [wb.k, mstr_bf.k], [wb.k])
                        mm(bO[:], Vh[:, j, :], wb[:], j == jmax, j == 0, [Vh.k, wb.k], [bO.k])
                        if j == 0:
                            so, ssem = stgb.next()
                            cp("dve", so[:], bO[:], [bO.k], [so.k])
                            store(MIX[h * P:(h + 1) * P, q * 512:(q + 1) * 512], so[:], so.k, ssem)

                    NBK = len(blocks)
                    for n in range(NBK + 2):
                        for s_ in range(2):
                            if n < NBK:
                                stageA(n, s_)
                        for s_ in range(2):
                            if 0 <= n - 1 < NBK:
                                stageB1(n - 1, s_)
                        for s_ in range(2):
                            if 0 <= n - 2 < NBK:
                                stageB2(n - 2, s_)
                kb.barrier()
            if stop_after == "C1":
                return
            with ExitStack() as ph:
                psb = mk_alloc(ph)
                qr = Ring(kb, psb, 2, [P, S], BF16, "qrm", sems=dsems[0:2])
                kr = Ring(kb, psb, 2, [P, S], BF16, "krm", sems=dsems[2:4])
                vr = Ring(kb, psb, 2, [P, 16, P], BF16, "vrm", sems=dsems[4:6])
                gqb = Tile(psb([P, P], F32, "gqb"))
                gkb = Tile(psb([P, P], F32, "gkb"))
                gmx = Tile(psb([P, 2], F32, "gmx"))
                negm = Tile(psb([P, 1], F32, "negm"))
                kb.dma("sp", gqb[:], gq_bc, misc_sem, writes=[gqb.k])
                kb.dma("sp", gkb[:], gk_bc, misc_sem, writes=[gkb.k])
                kb.op("dve", lambda e: e.tensor_reduce(out=gmx[:, 0:1], in_=gqb[:], axis=AX.X, op=ALU.max,
                                                        apply_absolute_value=True), [gqb.k], [gmx.k])
                kb.op("dve", lambda e: e.tensor_reduce(out=gmx[:, 1:2], in_=gkb[:], axis=AX.X, op=ALU.max,
                                                        apply_absolute_value=True), [gkb.k], [gmx.k])
                tt("dve", negm[:], gmx[:, 0:1], gmx[:, 1:2], ALU.mult, [gmx.k], [negm.k])
                ts("dve", negm[:], negm[:], -math.sqrt(128.0), None, ALU.mult, None, [negm.k], [negm.k])
                mincl_bf = Tile(psb([P, 4, 512], BF16, "mincl_bf"))
                m32 = Tile(psb([P, 4, 512], F32, "m32i"))
                kb.dma("sp", m32[:], cst_in[:, MS0 + 2048:MS0 + 4096].rearrange("p (r t) -> p r t", r=4), misc_sem, writes=[m32.k])
                cp("dve", mincl_bf[:], m32[:], [m32.k], [mincl_bf.k])
                Q32 = Tile(psb([P, S], F32, "Q32"))
                K32 = Tile(psb([P, S], F32, "K32"))
                km = Tile(psb([P, 8], F32, "km"))
                gmks = [Tile(psb([P, 8], F32, f"gmk{i}")) for i in range(4)]
                mx8s = [Tile(psb([P, 8], F32, f"mx8{i}")) for i in range(4)]
                sels = [Tile(psb([P, 8], F32, f"sel{i}")) for i in range(4)]
                selT = Tile(psb([8, 4, 512], BF16, "selT"))
                p32s = [Tile(psb([P, 512], F32, f"p32{i}")) for i in range(3)]
                pbs = [Tile(psb([P, 512], BF16, f"pb{i}")) for i in range(3)]
                rd = Tile(psb([P, 512], F32, "rd"))
                vtv = VT.rearrange("(kt k) e -> k kt e", k=P)
                it = 0
                for hd in range(4):
                    row0 = 1536 + hd * P
                    QTh, s0 = qr.next()
                    KTh, s1 = kr.next()
                    Vh, s2 = vr.next()
                    kb.dma("sp", QTh[:], QT[row0:row0 + P, :], s0, writes=[QTh.k])
                    kb.dma("sp", KTh[:], KT[row0:row0 + P, :], s1, writes=[KTh.k])
                    kb.dma("sp", Vh[:], vtv[:, :, row0:row0 + P], s2, writes=[Vh.k])
                    cp("act", Q32[:], QTh[:], [QTh.k], [Q32.k])
                    cp("pool", K32[:], KTh[:], [KTh.k], [K32.k])
                    kb.op("dve", lambda e: e.tensor_reduce(out=km[:], in_=K32[:].rearrange("p (n k) -> p n k", k=256),
                                                            axis=AX.X, op=ALU.add), [K32.k], [km.k])
                    for q in range(4):
                        bSel = banks[0]
                        bG = nbank6()
                        for tl in range(4):
                            t16 = 4 * q + tl
                            mm(bG[:, tl * 8:(tl + 1) * 8], Q32[:, t16 * P:(t16 + 1) * P], km[:], True, True, [Q32.k, km.k], [bG.k])
                        for tl in range(4):
                            t16 = 4 * q + tl
                            gmk1, mx81, sel1 = gmks[tl], mx8s[tl], sels[tl]
                            tt("dve", gmk1[:], bG[:, tl * 8:(tl + 1) * 8], negmask[:, t16, :], ALU.add, [bG.k, cst.k], [gmk1.k])
                            kb.op("dve", lambda e, a=mx81, b_=gmk1: e.max(out=a[:], in_=b_[:]), [gmk1.k], [mx81.k])
                            ts("dve", sel1[:], gmk1[:], mx81[:, 2:3], None, ALU.is_ge, None, [gmk1.k, mx81.k], [sel1.k])
                            tt("dve", sel1[:], sel1[:], validm[:, t16, :], ALU.mult, [sel1.k, cst.k], [sel1.k])
                            tt("dve", sel1[:], sel1[:], ownm[:, t16, :], ALU.add, [sel1.k, cst.k], [sel1.k])
                        for tl in range(4):
                            tp(bSel[0:8, tl * P:(tl + 1) * P], sels[tl][:], [sels[tl].k], [bSel.k])
                        cp("act", selT[:, q, :], bSel[0:8, :], [bSel.k], [selT.k])
                    mblocks = [(q, j) for q in range(4) for j in range(0, 4 * q + 4)]

                    def mA(n):
                        q, j = mblocks[n]
                        nblk = j // 2
                        p32 = p32s[n % 3]
                        bS = nbank6()
                        mm(bS[:], KTh[:, j * P:(j + 1) * P], QTh[:, q * 512:(q + 1) * 512], True, True, [KTh.k, QTh.k], [bS.k])
                        bM = nbank6()
                        mm(bM[:], oh_bf[0:8, nblk * P:(nblk + 1) * P], selT[:, q, :], True, True, [oh_bf.k, selT.k], [bM.k])
                        act(p32[:], bS[:], AF.Exp, [bS.k, negm.k], [p32.k], bias=negm[:, 0:1], scale=sc)
                        bMs[n] = bM

                    def mB(n):
                        q, j = mblocks[n]
                        jmax = 4 * q + 3
                        r = j - 4 * q
                        p32, pb = p32s[n % 3], pbs[n % 3]
                        bM = bMs.pop(n)
                        bO, bD = banks[0], banks[1]
                        tt("dve", pb[:], p32[:], bM[:], ALU.mult, [p32.k, bM.k], [pb.k])
                        if r >= 0:
                            tt("pool", pb[:], pb[:], mincl_bf[:, r, :], ALU.mult, [pb.k, mincl_bf.k], [pb.k])
                        mm(bO[:], Vh[:, j, :], pb[:], j == 0, j == jmax, [Vh.k, pb.k], [bO.k])
                        mm(bD[:], ones_bf[:], pb[:], j == 0, j == jmax, [ones_bf.k, pb.k], [bD.k])
                        if j == jmax:
                            recip(rd[:], bD[:], [bD.k], [rd.k])
                            so, ssem = stgb.next()
                            tt("dve", so[:], bO[:], rd[:], ALU.mult, [bO.k, rd.k], [so.k])
                            store(MIX[row0:row0 + P, q * 512:(q + 1) * 512], so[:], so.k, ssem)

                    bMs = {}
                    NM = len(mblocks)
                    for n in range(NM + 1):
                        if n < NM:
                            mA(n)
                        if n - 1 >= 0:
                            mB(n - 1)
                kb.barrier()
            if stop_after == "D1":
                return
            phase_out_ffn(1, X2, 16, cd_w_out, D, X1, outT)

        layer0()
        if stop_after is None or stop_after.endswith("1"):
            layer1()
        kb.barrier()
    return nc


def _consts():
    i = np.arange(P)
    ident = np.eye(P, dtype=np.float32)
    tri_le = (i[:, None] <= i[None, :]).astype(np.float32)
    sl_gt = (i[:, None] > i[None, :]).astype(np.float32)
    tri_ge = (i[:, None] >= i[None, :]).astype(np.float32)
    ones = np.ones((P, P), np.float32)
    t = np.arange(512)
    mstrict = np.stack([(t[None, :] > (r * P + i[:, None])) for r in range(4)], 1).astype(np.float32)
    mincl = np.stack([(t[None, :] >= (r * P + i[:, None])) for r in range(4)], 1).astype(np.float32)
    n = np.arange(8)
    negmask = np.zeros((P, 16, 8), np.float32)
    valid = np.zeros((P, 16, 8), np.float32)
    own = np.zeros((P, 16, 8), np.float32)
    for t16 in range(16):
        qb = t16 // 2
        negmask[:, t16, :] = np.where(n < qb, 0.0, -1e30)[None, :]
        valid[:, t16, :] = (n < qb).astype(np.float32)[None, :]
        own[:, t16, :] = (n == qb).astype(np.float32)[None, :]
    return np.concatenate([ident, tri_le, sl_gt, tri_ge, ones,
                           negmask.reshape(P, -1), valid.reshape(P, -1), own.reshape(P, -1),
                           mstrict.reshape(P, -1), mincl.reshape(P, -1)], axis=1).astype(np.float32)


def _col(v, nch):
    return np.ascontiguousarray(np.asarray(v, np.float32).reshape(nch, P).T)


def make_in_maps(x, c, norm_mix_g, norm_ffn_g, ada_w, ada_b, ffn_w_gate, ffn_w_up, ffn_w_down,
                 ab_w_in, ab_w_out, gm_ln_g, gm_ln_b, gm_w_s, gm_b_s, ssd_conv_w, ssd_conv_b,
                 ssd_dt_bias, ssd_a_log, ssd_d, ssd_norm_g, cd_w_in, cd_w_out, moba_q_norm_g, moba_k_norm_g):
    f = lambda a: np.ascontiguousarray(np.asarray(a, np.float32))
    bc = lambda v: np.ascontiguousarray(np.broadcast_to(np.asarray(v, np.float32).reshape(1, -1), (P, np.asarray(v).size)))
    shared = {
        "ada_w": f(ada_w),
        "ada_b": np.stack([_col(ada_b[l], 96) for l in range(2)]),
        "gmix": np.stack([_col(norm_mix_g[l], DC) for l in range(2)]),
        "gffn": np.stack([_col(norm_ffn_g[l], DC) for l in range(2)]),
        "ffn_w_gate": f(ffn_w_gate), "ffn_w_up": f(ffn_w_up), "ffn_w_down": f(ffn_w_down),
        "ab_w_in": f(ab_w_in[0]), "ab_w_out": f(ab_w_out[0]),
        "cd_w_in": f(cd_w_in[0]), "cd_w_out": f(cd_w_out[0]),
        "lng": _col(gm_ln_g[0], DC), "lnb": _col(gm_ln_b[0], DC),
        "wsT": np.ascontiguousarray(np.transpose(f(gm_w_s[0]), (2, 0, 1)).reshape(P, DC * P)),
        "bsbc": bc(f(gm_b_s[0]).reshape(-1)),
        "cw": np.ascontiguousarray(np.transpose(f(ssd_conv_w[0]).reshape(4, 24, P), (2, 1, 0)).reshape(P, 96)),
        "cb": _col(ssd_conv_b[0], 24),
        "dtb": bc(ssd_dt_bias[0]), "alog": bc(ssd_a_log[0]),
        "dbc": bc(np.repeat(f(ssd_d[0]), 64)), "sgbc": bc(ssd_norm_g[0]),
        "gq_col": f(moba_q_norm_g[0]).reshape(P, 1).copy(), "gk_col": f(moba_k_norm_g[0]).reshape(P, 1).copy(),
        "gq_bc": bc(moba_q_norm_g[0]), "gk_bc": bc(moba_k_norm_g[0]),
        "cst": _consts(),
    }
    maps = []
    for b in range(8):
        m = dict(shared)
        m["xT"] = np.ascontiguousarray(f(x[b]).T)
        m["c_col"] = _col(c[b], DC)
        maps.append(m)
    return maps


_NC_CACHE = {}


def kernel(**inputs):
    if "nc" not in _NC_CACHE:
        _NC_CACHE["nc"] = build_program()
    nc = _NC_CACHE["nc"]
    in_maps = make_in_maps(**inputs)
    res = run_bass_kernel_spmd(nc, in_maps, core_ids=list(range(8)))
    out = np.stack([np.asarray(r["outT"], dtype=np.float32).T for r in res.results], axis=0)
    return np.ascontiguousarray(out)
```

```python
import math
from contextlib import ExitStack

import numpy as np
import concourse.bass as bass
import concourse.mybir as mybir
from concourse.bass_utils import run_bass_kernel_spmd

F32 = mybir.dt.float32
BF16 = mybir.dt.bfloat16
AF = mybir.ActivationFunctionType
ALU = mybir.AluOpType
AX = mybir.AxisListType

ENGS = ("pe", "act", "dve", "pool", "sp")
P = 128
S = 2048
D = 2048
DC = 16
DFF = 5632
FC = 44
ABIN = 9248


class Sem:
    def __init__(self, handle, name):
        self.h = handle
        self.name = name
        self.val = 0


class Trk:
    __slots__ = ("w", "r")

    def __init__(self):
        self.w = None
        self.r = []


class Tile:
    def __init__(self, t):
        self.t = t
        self.k = Trk()

    def __getitem__(self, idx):
        return self.t[idx]


class KB:
    def __init__(self, nc, stack):
        self.nc = nc
        self.stack = stack
        self.eng = {"pe": nc.tensor, "act": nc.scalar, "dve": nc.vector,
                    "pool": nc.gpsimd, "sp": nc.sync}
        self.all_sems = []
        self.esem = {e: self.sem("e_" + e) for e in ENGS}
        self.seen = {e: {} for e in ENGS}
        self.n_inst = 0

    def sem(self, name):
        h = self.stack.enter_context(self.nc.semaphore(name))
        s = Sem(h, name)
        self.all_sems.append(s)
        return s

    def _wait(self, e, sem, val):
        if getattr(sem, "shared", False):
            val = sem.val
        seen = self.seen[e]
        if seen.get(sem, 0) >= val:
            return
        seen[sem] = val
        self.eng[e].wait_ge(sem.h, val)

    def _deps(self, e, reads, writes):
        for t in reads:
            if t.w is not None:
                s, v, we = t.w
                self._wait(e, s, v)
        for t in writes:
            if t.w is not None:
                s, v, we = t.w
                if not (e == "pe" and we == "pe"):
                    self._wait(e, s, v)
            for (s, v, re) in t.r:
                if re == e and s is self.esem[e]:
                    continue
                self._wait(e, s, v)

    def _mark(self, e, sem, val, reads, writes):
        for t in reads:
            t.r.append((sem, val, e))
            if len(t.r) > 64:
                t.r = t.r[-48:]
        for t in writes:
            t.w = (sem, val, e)
            t.r = []

    def op(self, e, fn, reads=(), writes=()):
        self._deps(e, reads, writes)
        inst = fn(self.eng[e])
        s = self.esem[e]
        s.val += 1
        inst.then_inc(s.h, 1)
        self._mark(e, s, s.val, reads, writes)
        self.n_inst += 1
        return inst

    def dma(self, e, out, in_, sem, reads=(), writes=()):
        self._deps(e, reads, writes)
        inst = self.eng[e].dma_start(out=out, in_=in_)
        sem.val += 16
        inst.then_inc(sem.h, 16)
        self._mark(e, sem, sem.val, reads, writes)
        self.n_inst += 1
        return inst

    def barrier(self):
        for e in ENGS:
            for s in self.all_sems:
                if s.val > 0:
                    self._wait(e, s, s.val)


class Ring:
    def __init__(self, kb, alloc, n, shape, dt, name, sems=None):
        self.slots = [Tile(alloc(shape, dt, f"{name}{i}")) for i in range(n)]
        self.sems = sems if sems is not None else [kb.sem(f"{name}_s{i}") for i in range(n)]
        self.i = 0
        self.n = n

    def next(self):
        s = self.slots[self.i % self.n]
        sem = self.sems[self.i % self.n]
        self.i += 1
        return s, sem


def build_program(dbg=None, stop_after=None, skip=(), e0step=99, nchunk=16):
    dbg = dbg or ()
    nc = bass.Bass("TRN2", target_bir_lowering=False)

    def din(name, shape, dt=F32):
        return nc.dram_tensor(name, list(shape), dt, kind="ExternalInput").ap()

    def dscr(name, shape, dt):
        kind = "ExternalOutput" if name in dbg else "Internal"
        return nc.dram_tensor(name, list(shape), dt, kind=kind).ap()

    xT = din("xT", [D, S])
    c_col = din("c_col", [P, DC])
    ada_w = din("ada_w", [2, D, 6 * D])
    ada_b = din("ada_b", [2, P, 96])
    gmix = din("gmix", [2, P, DC])
    gffn = din("gffn", [2, P, DC])
    w_gate = din("ffn_w_gate", [2, D, DFF])
    w_up = din("ffn_w_up", [2, D, DFF])
    w_down = din("ffn_w_down", [2, DFF, D])
    ab_w_in = din("ab_w_in", [D, ABIN])
    ab_w_out = din("ab_w_out", [2 * D, D])
    cd_w_in = din("cd_w_in", [D, 3 * D])
    cd_w_out = din("cd_w_out", [D, D])
    lng = din("lng", [P, DC])
    lnb = din("lnb", [P, DC])
    wsT_in = din("wsT", [P, DC * P])
    bsbc_in = din("bsbc", [P, DC * P])
    cw_in = din("cw", [P, 24 * 4])
    cb_in = din("cb", [P, 24])
    dtb_in = din("dtb", [P, 32])
    alog_in = din("alog", [P, 32])
    dbc_in = din("dbc", [P, D])
    sgbc_in = din("sgbc", [P, D])
    gq_col = din("gq_col", [P, 1])
    gk_col = din("gk_col", [P, 1])
    gq_bc = din("gq_bc", [P, P])
    gk_bc = din("gk_bc", [P, P])
    cst_in = din("cst", [P, 5 * P + 8 * 512 + 3 * 128])
    outT = nc.dram_tensor("outT", [D, S], F32, kind="ExternalOutput").ap()

    UT = dscr("UT", [D, S], BF16)
    VTOK = dscr("VTOK", [S, D], F32)
    SZ = dscr("SZ", [S, D], F32)
    XBCT = dscr("XBCT", [3072, S], F32)
    XCT = dscr("XCT", [3072, S], F32)
    DTs = dscr("DT", [S, 32], F32)
    MIX = dscr("MIX", [2 * D, S], BF16)
    X1 = dscr("X1", [D, S], F32)
    X2 = dscr("X2", [D, S], F32)
    AT = dscr("AT", [DFF, S], BF16)
    QT = dscr("QT", [D, S], BF16)
    KT = dscr("KT", [D, S], BF16)
    VT = dscr("VT", [S, D], BF16)
    MODD = dscr("MODD", [2, P, 96], F32)

    top = ExitStack()
    with top:
        kb = KB(nc, top)

        uid = [0]

        def mk_alloc(st):
            def sb(shape, dt, name):
                uid[0] += 1
                return st.enter_context(nc.sbuf_tensor(f"sb{uid[0]}_{name}", list(shape), dt))
            return sb

        gsb = mk_alloc(top)

        cst = Tile(gsb([P, 5 * P + 3 * 128], F32, "cst"))
        ident = cst[:, 0:128]
        tri_le = cst[:, 128:256]
        sl_gt = cst[:, 256:384]
        tri_ge = cst[:, 384:512]
        ones32 = cst[:, 512:640]
        o1 = 640
        MS0 = 1024
        negmask = cst[:, o1:o1 + 128].rearrange("p (t n) -> p t n", n=8)
        validm = cst[:, o1 + 128:o1 + 256].rearrange("p (t n) -> p t n", n=8)
        ownm = cst[:, o1 + 256:o1 + 384].rearrange("p (t n) -> p t n", n=8)
        ones_bf = Tile(gsb([P, P], BF16, "ones_bf"))
        trige_bf = Tile(gsb([P, P], BF16, "trige_bf"))
        oh_bf = Tile(gsb([P, 8 * P], BF16, "oh_bf"))
        modv = [Tile(gsb([P, 96], F32, f"mod{l}")) for l in range(2)]
        A1 = [Tile(gsb([P, DC], F32, f"A1_{l}")) for l in range(2)]
        G1 = [Tile(gsb([P, DC], F32, f"G1_{l}")) for l in range(2)]
        A2 = [Tile(gsb([P, DC], F32, f"A2_{l}")) for l in range(2)]
        G2 = [Tile(gsb([P, DC], F32, f"G2_{l}")) for l in range(2)]
        gm_t = Tile(gsb([P, 2 * DC], F32, "gm_t"))
        gf_t = Tile(gsb([P, 2 * DC], F32, "gf_t"))
        adab_t = Tile(gsb([P, 2 * 96], F32, "adab_t"))
        ccol = Tile(gsb([P, DC], F32, "ccol"))
        cact2 = Tile(gsb([P, DC, 2], F32, "cact2"))

        banks = [Tile(top.enter_context(nc.psum_tensor(f"bank{i}", [P, 512], F32))) for i in range(8)]
        bank_i = [0]

        def nbank():
            b = banks[bank_i[0] % 8]
            bank_i[0] += 1
            return b

        rot6 = [0]

        def nbank6():
            b = banks[2 + rot6[0] % 6]
            rot6[0] += 1
            return b

        dsems = [kb.sem(f"d{i}") for i in range(16)]
        wring = Ring(kb, gsb, 3, [P, 8192], BF16, "wr")
        stg = Ring(kb, gsb, 4, [P, 512], F32, "stg")
        stgb = Ring(kb, gsb, 4, [P, 512], BF16, "stgb")
        misc_sem = kb.sem("misc")
        misc_sem.shared = True

        def mm(out, lhsT, rhs, start, stop, reads, writes):
            kb.op("pe", lambda e: e.matmul(out, lhsT, rhs, start=start, stop=stop), reads, writes)

        def tp(out, in_, reads, writes):
            kb.op("pe", lambda e: e.matmul(out, in_, ident, start=True, stop=True), list(reads) + [cst.k], writes)

        def act(out, in_, func, reads, writes, bias=None, scale=None, accum=None):
            kw = {}
            if bias is not None:
                kw["bias"] = bias
            if scale is not None:
                kw["scale"] = scale
            if accum is not None:
                kw["accum_out"] = accum
            kb.op("act", lambda e: e.activation(out=out, in_=in_, func=func, **kw), reads, writes)

        def tt(eng, out, in0, in1, op, reads, writes):
            kb.op(eng, lambda e: e.tensor_tensor(out=out, in0=in0, in1=in1, op=op), reads, writes)

        def ts(eng, out, in0, s1, s2, op0, op1, reads, writes):
            if s2 is None:
                kb.op(eng, lambda e: e.tensor_scalar(out=out, in0=in0, scalar1=s1, scalar2=None, op0=op0), reads, writes)
            else:
                kb.op(eng, lambda e: e.tensor_scalar(out=out, in0=in0, scalar1=s1, scalar2=s2, op0=op0, op1=op1), reads, writes)

        def stt(out, in0, scalar, in1, op0, op1, reads, writes):
            kb.op("dve", lambda e: e.scalar_tensor_tensor(out=out, in0=in0, scalar=scalar, in1=in1, op0=op0, op1=op1), reads, writes)

        def recip(out, in_, reads, writes):
            kb.op("dve", lambda e: e.reciprocal(out=out, in_=in_), reads, writes)

        def cp(eng, out, in_, reads, writes):
            if eng == "act":
                kb.op("act", lambda e: e.copy(out=out, in_=in_), reads, writes)
            else:
                kb.op(eng, lambda e: e.tensor_copy(out=out, in_=in_), reads, writes)

        def memset(eng, ap, val, writes):
            kb.op(eng, lambda e: e.memset(ap, val), (), writes)

        def store(dst, tile_ap, tilek, sem):
            kb.dma("sp", dst, tile_ap, sem, reads=[tilek])

        def wload(W2d, KC, c0, gw, k0=0):
            slot, sem = wring.next()
            v = slot[:, 0:KC * gw].rearrange("p (k n) -> p k n", k=KC)
            src = W2d.rearrange("(kc p) n -> p kc n", p=P)[:, k0:k0 + KC, c0:c0 + gw]
            kb.dma("pool", v, src, sem, writes=[slot.k])
            return slot, v

        kb.dma("sp", cst[:], cst_in[:, 0:1024], misc_sem, writes=[cst.k])
        kb.dma("sp", ccol[:], c_col, misc_sem, writes=[ccol.k])
        kb.dma("sp", gm_t[:].rearrange("p (l c) -> p l c", l=2), gmix.rearrange("l p c -> p l c"), misc_sem, writes=[gm_t.k])
        kb.dma("sp", gf_t[:].rearrange("p (l c) -> p l c", l=2), gffn.rearrange("l p c -> p l c"), misc_sem, writes=[gf_t.k])
        kb.dma("sp", adab_t[:].rearrange("p (l c) -> p l c", l=2), ada_b.rearrange("l p c -> p l c"), misc_sem, writes=[adab_t.k])
        cp("dve", ones_bf[:], ones32, [cst.k], [ones_bf.k])
        cp("dve", trige_bf[:], tri_ge, [cst.k], [trige_bf.k])
        memset("dve", oh_bf[:], 0.0, [oh_bf.k])
        for n in range(8):
            cp("dve", oh_bf[:, n * P:(n + 1) * P], ident[:, n:n + 1].to_broadcast([P, P]), [cst.k], [oh_bf.k])
        act(cact2[:, :, 0], ccol[:], AF.Silu, [ccol.k], [cact2.k])
        act(cact2[:, :, 1], ccol[:], AF.Silu, [ccol.k], [cact2.k])

        def ada_group(l, fg, ring, q="sp"):
            bk = banks[l]
            wv = ada_w[l].rearrange("(kc p) f -> p kc f", p=P)
            slot, sem = ring.next()
            kb.dma(q, slot[:], wv[:, :, fg * 512:(fg + 1) * 512], sem, writes=[slot.k])
            for fl in range(4):
                col = fg * 4 + fl
                for kc in range(DC):
                    mm(bk[:, 2 * col:2 * col + 2], slot[:, kc, fl * P:(fl + 1) * P], cact2[:, kc, :],
                       kc == 0, kc == DC - 1, [slot.k, cact2.k], [bk.k])

        def ada_finish(l):
            bk = banks[l]
            tt("dve", modv[l][:], bk[:, 0:192].rearrange("p (c two) -> p c two", two=2)[:, :, 0],
               adab_t[:, l * 96:(l + 1) * 96], ALU.add, [bk.k, adab_t.k], [modv[l].k])
            m = modv[l]
            stt(A1[l][:], m[:, 16:32], 1.0, gm_t[:, l * DC:(l + 1) * DC], ALU.add, ALU.mult, [m.k, gm_t.k], [A1[l].k])
            ts("dve", G1[l][:], m[:, 32:48], 1.0, None, ALU.add, None, [m.k], [G1[l].k])
            stt(A2[l][:], m[:, 64:80], 1.0, gf_t[:, l * DC:(l + 1) * DC], ALU.add, ALU.mult, [m.k, gf_t.k], [A2[l].k])
            ts("dve", G2[l][:], m[:, 80:96], 1.0, None, ALU.add, None, [m.k], [G2[l].k])
            if "MODD" in dbg:
                store(MODD[l], m[:], m.k, misc_sem)

        with ExitStack() as ph:
            psb = mk_alloc(ph)
            aring = Ring(kb, psb, 2, [P, DC, 512], F32, "ada", sems=dsems[0:2])
            for fg in range(24 if "P0" not in skip else 1):
                ada_group(0, fg, aring)
            ada_finish(0)
            kb.barrier()
        B1 = [modv[l][:, 0:16] for l in range(2)]
        B2 = [modv[l][:, 48:64] for l in range(2)]

        def norm_chunk(sb_tmp, Xc, Acol, Bcol, modk, out_fn, outk):
            sq, rs, rstd, tmps = sb_tmp
            bk = nbank()
            for dc in range(DC):
                sq1 = sq[dc % 2]
                act(sq1[:], Xc[:, dc, :], AF.Square, [Xc.k], [sq1.k])
                mm(bk[:], ones_bf[:], sq1[:], dc == 0, dc == DC - 1, [ones_bf.k, sq1.k], [bk.k])
            act(rs[:], bk[:], AF.Sqrt, [bk.k], [rs.k], bias=1e-6, scale=1.0 / D)
            recip(rstd[:], rs[:], [rs.k], [rstd.k])
            for dc in range(DC):
                tmp = tmps[dc % 2]
                stt(tmp[:], Xc[:, dc, :], Acol[:, dc:dc + 1], rstd[:], ALU.mult, ALU.mult,
                    [Xc.k, modk[0], rstd.k], [tmp.k])
                act(out_fn(dc), tmp[:], AF.Identity, [tmp.k, modk[1]], [outk], bias=Bcol[:, dc:dc + 1], scale=1.0)

        def norm_tmp(psb, tag):
            return ([Tile(psb([P, 512], BF16, f"sq{tag}{i}")) for i in range(2)], Tile(psb([P, 512], F32, "rs" + tag)),
                    Tile(psb([P, 512], F32, "rstd" + tag)),
                    [Tile(psb([P, 512], F32, f"ntmp{tag}{i}")) for i in range(2)])

        def phase_norm_full(l, src, HM):
            with ExitStack() as ph:
                psb = mk_alloc(ph)
                xr = Ring(kb, psb, 1, [P, DC, 512], F32, "xr", sems=dsems[0:1])
                ntmp = norm_tmp(psb, "a")
                sv = src.rearrange("(c p) t -> p c t", p=P)
                for tc in range(4):
                    Xc, sem = xr.next()
                    kb.dma("sp", Xc[:], sv[:, :, tc * 512:(tc + 1) * 512], sem, writes=[Xc.k])
                    norm_chunk(ntmp, Xc, A1[l], B1[l], (A1[l].k, modv[l].k),
                               lambda dc: HM[:, dc, tc * 512:(tc + 1) * 512], HM.k)
                kb.barrier()

        def proj_fm(W2d, c0, ncols, HM, evac):
            if "PROJ" in skip:
                return
            for g0 in range(0, ncols, 512):
                slot, wv = wload(W2d, DC, c0 + g0, 512)
                for ocl in range(4):
                    for tc in range(4):
                        bk = nbank()
                        for kc in range(DC):
                            mm(bk[:], wv[:, kc, ocl * P:(ocl + 1) * P], HM[:, kc, tc * 512:(tc + 1) * 512],
                               kc == 0, kc == DC - 1, [slot.k, HM.k], [bk.k])
                        evac(g0 // P + ocl, tc, bk)

        def proj_tm(W2d, c0, ncols, HM, evac):
            if "PROJ" in skip:
                return
            for g0 in range(0, ncols, 512):
                gw = min(512, ncols - g0)
                slot, wv = wload(W2d, DC, c0 + g0, gw)
                for t16 in range(16):
                    bk = nbank()
                    for kc in range(DC):
                        mm(bk[:, 0:gw], HM[:, kc, t16 * P:(t16 + 1) * P], wv[:, kc, :],
                           kc == 0, kc == DC - 1, [slot.k, HM.k], [bk.k])
                    evac(t16, g0, gw, bk)

        def phase_out_ffn(l, xsrc, KCm, w_out2d, mixrows, xdst_mid, xdst_final):
            with ExitStack() as outer:
                osb = mk_alloc(outer)
                HF = Tile(osb([P, DC, S], BF16, "HF"))
                with ExitStack() as ph:
                    psb = mk_alloc(ph)
                    Xt = Tile(psb([P, DC, 512], F32, "Xt"))
                    Mt = Tile(psb([P, KCm, 512], BF16, "Mt"))
                    ntmp = norm_tmp(psb, "f")
                    gw = 256 if KCm == 32 else 512
                    xv = xsrc.rearrange("(c p) t -> p c t", p=P)
                    mv = MIX[0:mixrows, :].rearrange("(c p) t -> p c t", p=P)
                    x1v = xdst_mid.rearrange("(c p) t -> p c t", p=P)
                    for tc in range(4):
                        tsl = slice(tc * 512, (tc + 1) * 512)
                        kb.dma("sp", Xt[:], xv[:, :, tsl], dsems[0], writes=[Xt.k])
                        kb.dma("sp", Mt[:], mv[:, :, tsl], dsems[1], writes=[Mt.k])
                        for g0 in range(0, D, gw):
                            slot, wv = wload(w_out2d, KCm, g0, gw)
                            for ocl in range(gw // P):
                                dc = g0 // P + ocl
                                bk = nbank()
                                for kc in range(KCm):
                                    mm(bk[:], wv[:, kc, ocl * P:(ocl + 1) * P], Mt[:, kc, :],
                                       kc == 0, kc == KCm - 1, [slot.k, Mt.k], [bk.k])
                                stt(Xt[:, dc, :], bk[:], G1[l][:, dc:dc + 1], Xt[:, dc, :], ALU.mult, ALU.add,
                                    [bk.k, G1[l].k, Xt.k], [Xt.k])
                        kb.dma("sp", x1v[:, :, tsl], Xt[:], dsems[2], reads=[Xt.k])
                        norm_chunk(ntmp, Xt, A2[l], B2[l], (A2[l].k, modv[l].k),
                                   lambda dc: HF[:, dc, tsl], HF.k)
                    kb.barrier()
                if stop_after == f"F{l}":
                    return
                with ExitStack() as ph:
                    psb = mk_alloc(ph)
                    sgs = [Tile(psb([P, 512], F32, f"sg{i}")) for i in range(2)]
                    cnt = 0
                    for fg in range(11):
                        sg_slot, wg = wload(w_gate[l], DC, fg * 512, 512)
                        su_slot, wu = wload(w_up[l], DC, fg * 512, 512)
                        for fl in range(4):
                            fc = fg * 4 + fl
                            for tc in range(4):
                                tsl = slice(tc * 512, (tc + 1) * 512)
                                bg = nbank()
                                bu = nbank()
                                for kc in range(DC):
                                    mm(bg[:], wg[:, kc, fl * P:(fl + 1) * P], HF[:, kc, tsl], kc == 0, kc == DC - 1,
                                       [sg_slot.k, HF.k], [bg.k])
                                for kc in range(DC):
                                    mm(bu[:], wu[:, kc, fl * P:(fl + 1) * P], HF[:, kc, tsl], kc == 0, kc == DC - 1,
                                       [su_slot.k, HF.k], [bu.k])
                                sg = sgs[cnt % 2]
                                cnt += 1
                                act(sg[:], bg[:], AF.Silu, [bg.k], [sg.k])
                                so, ssem = stgb.next()
                                tt("dve", so[:], sg[:], bu[:], ALU.mult, [sg.k, bu.k], [so.k])
                                store(AT[fc * P:(fc + 1) * P, tsl], so[:], so.k, ssem)
                    kb.barrier()
            if stop_after == f"G{l}":
                return
            with ExitStack() as ph:
                psb = mk_alloc(ph)
                At = Tile(psb([P, FC, 1024], BF16, "At"))
                xgr = Ring(kb, psb, 2, [P, 2, 1024], F32, "xgr", sems=dsems[2:4])
                x1v = xdst_mid.rearrange("(c p) t -> p c t", p=P)
                av = AT.rearrange("(c p) t -> p c t", p=P)
                ov = xdst_final.rearrange("(c p) t -> p c t", p=P)
                for tp2 in range(2):
                    tsl = slice(tp2 * 1024, (tp2 + 1) * 1024)
                    kb.dma("sp", At[:], av[:, :, tsl], dsems[1], writes=[At.k])
                    for g0 in range(0, D, 256):
                        dc0 = g0 // P
                        Xg, xsem = xgr.next()
                        kb.dma("sp", Xg[:], x1v[:, dc0:dc0 + 2, tsl], xsem, writes=[Xg.k])
                        bks = [[nbank(), nbank()], [nbank(), nbank()]]
                        for half in range(2):
                            slot, wv = wload(w_down[l], 22, g0, 256, k0=22 * half)
                            for ocl in range(2):
                                for tl in range(2):
                                    bk = bks[ocl][tl]
                                    for kc in range(22):
                                        mm(bk[:], wv[:, kc, ocl * P:(ocl + 1) * P],
                                           At[:, 22 * half + kc, tl * 512:(tl + 1) * 512],
                                           half == 0 and kc == 0, half == 1 and kc == 21, [slot.k, At.k], [bk.k])
                        for ocl in range(2):
                            dc = dc0 + ocl
                            for tl in range(2):
                                bk = bks[ocl][tl]
                                xs_ = Xg[:, ocl, tl * 512:(tl + 1) * 512]
                                stt(xs_, bk[:], G2[l][:, dc:dc + 1], xs_, ALU.mult, ALU.add,
                                    [bk.k, G2[l].k, Xg.k], [Xg.k])
                        kb.dma("sp", ov[:, dc0:dc0 + 2, tsl], Xg[:], xsem, reads=[Xg.k])
                kb.barrier()

        def layer0():
            with ExitStack() as outer:
                osb = mk_alloc(outer)
                HM = Tile(osb([P, DC, S], BF16, "HM"))
                if "A0" not in skip:
                    phase_norm_full(0, xT, HM)
                if stop_after == "A0":
                    return
                with ExitStack() as ph:
                    if "B0" in skip:
                        raise_skip = True
                    else:
                        raise_skip = False
                    psb = mk_alloc(ph)
                    dtb = Tile(psb([P, 32], F32, "dtb"))
                    dtt = [Tile(psb([P, 32], F32, f"dtt{i}")) for i in range(2)]
                    kb.dma("sp", dtb[:], dtb_in, misc_sem, writes=[dtb.k])

                    def ev_u(oc, tc, bk):
                        so, ssem = stgb.next()
                        act(so[:], bk[:], AF.Gelu_apprx_tanh, [bk.k], [so.k])
                        store(UT[oc * P:(oc + 1) * P, tc * 512:(tc + 1) * 512], so[:], so.k, ssem)
                    proj_fm(ab_w_in, 0, 2048, HM, ev_u)

                    def ev_v(t16, g0, gw, bk):
                        so, ssem = stg.next()
                        act(so[:], bk[:], AF.Gelu_apprx_tanh, [bk.k], [so.k])
                        store(VTOK[t16 * P:(t16 + 1) * P, g0:g0 + gw], so[:], so.k, ssem)
                    proj_tm(ab_w_in, 2048, 2048, HM, ev_v)

                    def ev_z(t16, g0, gw, bk):
                        so, ssem = stg.next()
                        act(so[:], bk[:], AF.Silu, [bk.k], [so.k])
                        store(SZ[t16 * P:(t16 + 1) * P, g0:g0 + gw], so[:], so.k, ssem)
                    proj_tm(ab_w_in, 4096, 2048, HM, ev_z)

                    ecnt = [0]

                    def ev_x(oc, tc, bk):
                        so, ssem = stg.next()
                        cp("act" if ecnt[0] % 2 else "dve", so[:], bk[:], [bk.k], [so.k])
                        ecnt[0] += 1
                        store(XBCT[oc * P:(oc + 1) * P, tc * 512:(tc + 1) * 512], so[:], so.k, ssem)
                    proj_fm(ab_w_in, 6144, 3072, HM, ev_x)

                    def ev_dt(t16, g0, gw, bk):
                        d1 = dtt[t16 % 2]
                        so, ssem = stg.next()
                        tt("dve", d1[:], bk[:, 0:32], dtb[:], ALU.add, [bk.k, dtb.k], [d1.k])
                        act(so[:, 0:32], d1[:], AF.Softplus, [d1.k], [so.k])
                        store(DTs[t16 * P:(t16 + 1) * P, :], so[:, 0:32], so.k, ssem)
                    proj_tm(ab_w_in, 9216, 32, HM, ev_dt)
                    kb.barrier()
            if stop_after == "B0":
                return
            with ExitStack() as ph:
                psb = mk_alloc(ph)
                wsT32 = Tile(psb([P, DC, P], F32, "wsT32"))
                wsT = Tile(psb([P, DC, P], BF16, "wsT"))
                bsbc = Tile(psb([P, DC, P], F32, "bsbc"))
                T2 = Tile(psb([P, DC, P], F32, "T2"))
                lng_t = Tile(psb([P, DC], F32, "lng_t"))
                lnb_t = Tile(psb([P, DC], F32, "lnb_t"))
                kb.dma("sp", wsT32[:], wsT_in.rearrange("p (h t) -> p h t", h=DC), misc_sem, writes=[wsT32.k])
                kb.dma("sp", bsbc[:], bsbc_in.rearrange("p (h t) -> p h t", h=DC), misc_sem, writes=[bsbc.k])
                kb.dma("sp", lng_t[:], lng, misc_sem, writes=[lng_t.k])
                kb.dma("sp", lnb_t[:], lnb, misc_sem, writes=[lnb_t.k])
                tt("dve", wsT[:], wsT32[:], tri_le.unsqueeze(1).to_broadcast([P, DC, P]), ALU.mult,
                   [wsT32.k, cst.k], [wsT.k])
                for q in range(4):
                    bk = nbank()
                    mm(bk[:], ones_bf[:], wsT[:, 4 * q:4 * q + 4, :].rearrange("p h t -> p (h t)"), True, True,
                       [ones_bf.k, wsT.k], [bk.k])
                    for hl in range(4):
                        h = 4 * q + hl
                        stt(T2[:, h, :], bk[:, hl * P:(hl + 1) * P], lnb_t[:, h:h + 1], bsbc[:, h, :], ALU.mult, ALU.add,
                            [bk.k, lnb_t.k, bsbc.k], [T2.k])
                vr = Ring(kb, psb, 2, [P, D], F32, "vr", sems=dsems[0:2])
                ur = Ring(kb, psb, 2, [P, DC, P], BF16, "ur", sems=dsems[2:4])
                yr = Ring(kb, psb, 2, [P, DC, P], BF16, "yr", sems=dsems[4:6])
                nb = Tile(psb([P, D], BF16, "nb"))
                st6 = Tile(psb([P, 4, 6], F32, "st6"))
                mvv = Tile(psb([P, 2], F32, "mvv"))
                sd = Tile(psb([P, 1], F32, "sd"))
                rr = Tile(psb([P, 1], F32, "rr"))
                tmpa = [Tile(psb([P, 4, P], F32, f"tmpa{i}")) for i in range(2)]
                utv = UT.rearrange("(h p) t -> p h t", p=P)
                mxv = MIX[0:D, :].rearrange("(h p) t -> p h t", p=P)
                for t16 in range(16 if "C0" not in skip else 0):
                    tsl = slice(t16 * P, (t16 + 1) * P)
                    Vt, vsem = vr.next()
                    Ut, usem = ur.next()
                    Ya, ysem = yr.next()
                    kb.dma("sp", Vt[:], VTOK[tsl, :], vsem, writes=[Vt.k])
                    kb.dma("sp", Ut[:], utv[:, :, tsl], usem, writes=[Ut.k])
                    for j in range(4):
                        kb.op("dve", lambda e, j=j: e.bn_stats(out=st6[:, j, :], in_=Vt[:, j * 512:(j + 1) * 512]),
                              [Vt.k], [st6.k])
                    kb.op("dve", lambda e: e.bn_aggr(out=mvv[:], in_=st6[:].rearrange("p a b -> p (a b)")), [st6.k], [mvv.k])
                    act(sd[:], mvv[:, 1:2], AF.Sqrt, [mvv.k], [sd.k], bias=1e-5, scale=1.0)
                    recip(rr[:], sd[:], [sd.k], [rr.k])
                    ts("dve", nb[:], Vt[:], mvv[:, 0:1], rr[:], ALU.subtract, ALU.mult, [Vt.k, mvv.k, rr.k], [nb.k])
                    for q in range(4):
                        bk = nbank()
                        for hl in range(4):
                            h = 4 * q + hl
                            mm(bk[:, hl * P:(hl + 1) * P], nb[:, h * P:(h + 1) * P], wsT[:, h, :], True, True,
                               [nb.k, wsT.k], [bk.k])
                        tm = tmpa[q % 2]
                        for hl in range(4):
                            h = 4 * q + hl
                            stt(tm[:, hl, :], bk[:, hl * P:(hl + 1) * P], lng_t[:, h:h + 1], T2[:, h, :], ALU.mult, ALU.add,
                                [bk.k, lng_t.k, T2.k], [tm.k])
                        tt("dve", Ya[:, 4 * q:4 * q + 4, :], tm[:], Ut[:, 4 * q:4 * q + 4, :], ALU.mult,
                           [tm.k, Ut.k], [Ya.k])
                    kb.dma("sp", mxv[:, :, tsl], Ya[:], ysem, reads=[Ya.k])
                kb.barrier()
            if stop_after == "C0":
                return
            with ExitStack() as ph:
                psb = mk_alloc(ph)
                cw = Tile(psb([P, 24, 4], F32, "cw"))
                cb = Tile(psb([P, 24], F32, "cb"))
                kb.dma("sp", cw[:], cw_in.rearrange("p (c k) -> p c k", k=4), misc_sem, writes=[cw.k])
                kb.dma("sp", cb[:], cb_in, misc_sem, writes=[cb.k])
                xrr = Ring(kb, psb, 2, [P, S + 4], F32, "xrr", sems=dsems[0:2])
                accs = [Tile(psb([P, S], F32, f"acc{i}")) for i in range(2)]
                xo = Ring(kb, psb, 2, [P, S], F32, "xo", sems=dsems[2:4])
                aring1 = Ring(kb, psb, 2, [P, DC, 512], F32, "ada1", sems=dsems[4:6])
                for sl_ in xrr.slots:
                    memset("dve", sl_[:, 0:4], 0.0, [sl_.k])
                for cc in range(24 if "D0" not in skip else 0):
                    ada_group(1, cc, aring1, q="pool")
                    XR, xsem = xrr.next()
                    acc = accs[cc % 2]
                    XO, osem = xo.next()
                    kb.dma("sp", XR[:, 4:S + 4], XBCT[cc * P:(cc + 1) * P, :], xsem, writes=[XR.k])
                    ts("dve", acc[:], XR[:, 4:S + 4], cw[:, cc, 3:4], cb[:, cc:cc + 1], ALU.mult, ALU.add,
                       [XR.k, cw.k, cb.k], [acc.k])
                    for k in (2, 1, 0):
                        stt(acc[:], XR[:, 1 + k:S + 1 + k], cw[:, cc, k:k + 1], acc[:], ALU.mult, ALU.add,
                            [XR.k, cw.k, acc.k], [acc.k])
                    act(XO[:], acc[:], AF.Silu, [acc.k], [XO.k])
                    kb.dma("sp", XCT[cc * P:(cc + 1) * P, :], XO[:], osem, reads=[XO.k])
                ada_finish(1)
                kb.barrier()
            if stop_after == "D0":
                return
            with ExitStack() as ph:
                psb = mk_alloc(ph)
                S32 = Tile(psb([P, D], F32, "S32"))
                Sbf = Tile(psb([P, D], BF16, "Sbf"))
                abc = Tile(psb([P, 32], F32, "abc"))
                alog = Tile(psb([P, 32], F32, "alog"))
                Dbc = Tile(psb([P, D], F32, "Dbc"))
                gbc = Tile(psb([P, D], F32, "gbc"))
                kb.dma("sp", alog[:], alog_in, misc_sem, writes=[alog.k])
                kb.dma("sp", Dbc[:], dbc_in, misc_sem, writes=[Dbc.k])
                kb.dma("sp", gbc[:], sgbc_in, misc_sem, writes=[gbc.k])
                act(abc[:], alog[:], AF.Exp, [alog.k], [abc.k])
                ts("dve", abc[:], abc[:], -1.0, None, ALU.mult, None, [abc.k], [abc.k])
                memset("dve", S32[:], 0.0, [S32.k])
                memset("dve", Sbf[:], 0.0, [Sbf.k])
                xcr = Ring(kb, psb, 2, [P, 24, P], F32, "xcr", sems=dsems[0:2])
                szr = Ring(kb, psb, 2, [P, D], F32, "szr", sems=dsems[2:4])
                dtr = Ring(kb, psb, 2, [P, 32], F32, "dtr", sems=dsems[4:6])
                ybr = Ring(kb, psb, 2, [P, DC, P], BF16, "ybr", sems=dsems[6:8])
                adt = Tile(psb([P, 32], F32, "adt"))
                eac = Tile(psb([P, 64], F32, "eac"))
                xs_tok = Tile(psb([P, D], F32, "xs_tok"))
                xdt = Tile(psb([P, D], BF16, "xdt"))
                xdte = Tile(psb([P, D], BF16, "xdte"))
                Btok = Tile(psb([P, 4, P], BF16, "Btok"))
                BCT = Tile(psb([P, 8, P], BF16, "BCT"))
                mCB = Tile(psb([P, 4, P], F32, "mCB"))
                A4s = [Tile(psb([P, 4, P], F32, f"A4{i}")) for i in range(2)]
                E4s = [Tile(psb([P, 4, P], F32, f"E4{i}")) for i in range(2)]
                MTall = Tile(psb([P, 32, P], BF16, "MTall"))
                dte = Tile(psb([P, 32], F32, "dte"))
                t1 = Tile(psb([P, 512], F32, "t1"))
                t3 = Tile(psb([P, 512], F32, "t3"))
                yg = Tile(psb([P, D], F32, "yg"))
                ss = Tile(psb([P, 1], F32, "ss"))
                sd = Tile(psb([P, 1], F32, "sd2"))
                rr = Tile(psb([P, 1], F32, "rr2"))
                yn = Tile(psb([P, D], F32, "yn"))
                tS = Tile(psb([P, 512], F32, "tS"))
                xcv = XCT.rearrange("(c p) t -> p c t", p=P)
                mxv = MIX[D:2 * D, :].rearrange("(h p) t -> p h t", p=P)
                for c in range(nchunk):
                    tsl = slice(c * P, (c + 1) * P)
                    XC, s0 = xcr.next()
                    SZc, s1 = szr.next()
                    DTc, s2 = dtr.next()
                    YB, s3 = ybr.next()
                    kb.dma("sp", XC[:], xcv[:, :, tsl], s0, writes=[XC.k])
                    kb.dma("sp", SZc[:], SZ[tsl, :], s1, writes=[SZc.k])
                    kb.dma("sp", DTc[:], DTs[tsl, :], s2, writes=[DTc.k])
                    tt("dve", adt[:], DTc[:], abc[:], ALU.mult, [DTc.k, abc.k], [adt.k])
                    bA = nbank()
                    mm(bA[:, 0:32], tri_le, adt[:], True, True, [cst.k, adt.k], [bA.k])
                    mm(bA[:, 32:64], ones32, adt[:], True, True, [cst.k, adt.k], [bA.k])
                    act(eac[:], bA[:, 0:64], AF.Exp, [bA.k], [eac.k])
                    if e0step <= 1:
                        continue
                    for q in range(4):
                        bk = nbank()
                        for i in range(4):
                            tp(bk[:, i * P:(i + 1) * P], XC[:, 4 * q + i, :], [XC.k], [bk.k])
                        cp("act", xs_tok[:, q * 512:(q + 1) * 512], bk[:], [bk.k], [xs_tok.k])
                        if e0step <= 1.2:
                            continue
                        tt("dve", xdt[:, q * 512:(q + 1) * 512].rearrange("p (h j) -> p j h", j=64),
                           xs_tok[:, q * 512:(q + 1) * 512].rearrange("p (h j) -> p j h", j=64),
                           DTc[:, 8 * q:8 * q + 8].unsqueeze(1).to_broadcast([P, 64, 8]), ALU.mult,
                           [xs_tok.k, DTc.k], [xdt.k])
                    if e0step <= 1.4:
                        continue
                    bB = nbank()
                    for g in range(4):
                        tp(bB[:, g * P:(g + 1) * P], XC[:, 16 + g, :], [XC.k], [bB.k])
                    cp("act", Btok[:].rearrange("p g n -> p (g n)"), bB[:], [bB.k], [Btok.k])
                    if e0step <= 1.6:
                        continue
                    cp("pool", BCT[:], XC[:, 16:24, :], [XC.k], [BCT.k])
                    if e0step <= 2:
                        continue
                    bC = nbank()
                    for g in range(4):
                        mm(bC[:, g * P:(g + 1) * P], BCT[:, g, :], BCT[:, 4 + g, :], True, True, [BCT.k], [bC.k])
                    tt("dve", mCB[:], bC[:].rearrange("p (g t) -> p g t", g=4),
                       tri_le.unsqueeze(1).to_broadcast([P, 4, P]), ALU.mult, [bC.k, cst.k], [mCB.k])
                    if e0step <= 3:
                        continue
                    for g in range(4):
                        for half in range(2):
                            b8 = 2 * g + half
                            A4 = A4s[b8 % 2]
                            E4 = E4s[b8 % 2]
                            for hl in range(4):
                                ts("dve", A4[:, hl, :], sl_gt, adt[:, 4 * b8 + hl:4 * b8 + hl + 1], None, ALU.mult, None,
                                   [cst.k, adt.k], [A4.k])
                            bk = nbank()
                            for hl in range(4):
                                mm(bk[:, hl * P:(hl + 1) * P], A4[:, hl, :], tri_le, True, True, [A4.k, cst.k], [bk.k])
                            act(E4[:].rearrange("p h t -> p (h t)"), bk[:], AF.Exp, [bk.k], [E4.k])
                            tt("dve", MTall[:, 4 * b8:4 * b8 + 4, :], E4[:],
                               mCB[:, g:g + 1, :].to_broadcast([P, 4, P]), ALU.mult, [E4.k, mCB.k], [MTall.k])
                            cp("act", dte[:, 4 * b8:4 * b8 + 4], E4[:, :, P - 1], [E4.k], [dte.k])
                    for g in range(4):
                        bY = nbank()
                        for hh in range(8):
                            h = 8 * g + hh
                            mm(bY[:, hh * 64:(hh + 1) * 64], MTall[:, h, :], xdt[:, h * 64:(h + 1) * 64], True, True,
                               [MTall.k, xdt.k], [bY.k])
                        bO = nbank()
                        mm(bO[:], BCT[:, 4 + g, :], Sbf[:, g * 512:(g + 1) * 512], True, True, [BCT.k, Sbf.k], [bO.k])
                        gs = slice(g * 512, (g + 1) * 512)
                        tt("dve", t1[:].rearrange("p (h j) -> p j h", j=64), bO[:].rearrange("p (h j) -> p j h", j=64),
                           eac[:, 8 * g:8 * g + 8].unsqueeze(1).to_broadcast([P, 64, 8]), ALU.mult,
                           [bO.k, eac.k], [t1.k])
                        tt("pool", t3[:], xs_tok[:, gs], Dbc[:, gs], ALU.mult, [xs_tok.k, Dbc.k], [t3.k])
                        tt("dve", t1[:], t1[:], bY[:], ALU.add, [t1.k, bY.k], [t1.k])
                        tt("dve", t1[:], t1[:], t3[:], ALU.add, [t1.k, t3.k], [t1.k])
                        tt("dve", yg[:, gs], t1[:], SZc[:, gs], ALU.mult, [t1.k, SZc.k], [yg.k])
                    if e0step <= 4:
                        continue
                    tt("dve", xdte[:].rearrange("p (h j) -> p j h", j=64), xdt[:].rearrange("p (h j) -> p j h", j=64),
                       dte[:].unsqueeze(1).to_broadcast([P, 64, 32]), ALU.mult, [xdt.k, dte.k], [xdte.k])
                    for g in range(4):
                        gs = slice(g * 512, (g + 1) * 512)
                        bS = nbank()
                        mm(bS[:], Btok[:, g, :], xdte[:, gs], True, True, [Btok.k, xdte.k], [bS.k])
                        tt("dve", tS[:].rearrange("p (h j) -> p j h", j=64), S32[:, gs].rearrange("p (h j) -> p j h", j=64),
                           eac[:, 32 + 8 * g:32 + 8 * g + 8].unsqueeze(1).to_broadcast([P, 64, 8]), ALU.mult,
                           [S32.k, eac.k], [tS.k])
                        tt("dve", S32[:, gs], tS[:], bS[:], ALU.add, [tS.k, bS.k], [S32.k])
                        cp("pool", Sbf[:, gs], S32[:, gs], [S32.k], [Sbf.k])
                    if e0step <= 5:
                        continue
                    act(xdte[:], yg[:], AF.Square, [yg.k], [xdte.k, ss.k], accum=ss[:])
                    act(sd[:], ss[:], AF.Sqrt, [ss.k], [sd.k], bias=1e-6, scale=1.0 / D)
                    recip(rr[:], sd[:], [sd.k], [rr.k])
                    stt(yn[:], yg[:], rr[:], gbc[:], ALU.mult, ALU.mult, [yg.k, rr.k, gbc.k], [yn.k])
                    for q in range(4):
                        bk = nbank()
                        for i in range(4):
                            tp(bk[:, i * P:(i + 1) * P], yn[:, (4 * q + i) * P:(4 * q + i + 1) * P], [yn.k], [bk.k])
                        cp("act", YB[:, 4 * q:4 * q + 4, :].rearrange("p h t -> p (h t)"), bk[:], [bk.k], [YB.k])
                    kb.dma("sp", mxv[:, :, tsl], YB[:], s3, reads=[YB.k])
                kb.barrier()
            if stop_after == "E0":
                return
            phase_out_ffn(0, xT, 32, ab_w_out, 2 * D, X1, X2)

        def attn_loads(psb, qrow, vcol, tag):
            QTh = Tile(psb([P, S], BF16, "QTh" + tag))
            KTh = Tile(psb([P, S], BF16, "KTh" + tag))
            Vh = Tile(psb([P, 16, P], BF16, "Vh" + tag))
            return QTh, KTh, Vh

        def layer1():
            with ExitStack() as outer:
                osb = mk_alloc(outer)
                HM = Tile(osb([P, DC, S], BF16, "HM1"))
                phase_norm_full(1, X2, HM)
                if stop_after == "A1":
                    return
                with ExitStack() as ph:
                    psb = mk_alloc(ph)
                    gq = Tile(psb([P, 1], F32, "gq"))
                    gk = Tile(psb([P, 1], F32, "gk"))
                    kb.dma("sp", gq[:], gq_col, misc_sem, writes=[gq.k])
                    kb.dma("sp", gk[:], gk_col, misc_sem, writes=[gk.k])
                    sqb = [Tile(psb([P, 512], BF16, f"sqb{i}")) for i in range(2)]
                    rsb = [Tile(psb([P, 512], F32, f"rsb{i}")) for i in range(2)]
                    ecnt = [0]

                    def mk_ev_copy(dst, row0):
                        def ev(oc, tc, bk):
                            so, ssem = stgb.next()
                            cp("act" if ecnt[0] % 2 else "dve", so[:], bk[:], [bk.k], [so.k])
                            ecnt[0] += 1
                            store(dst[row0 + oc * P:row0 + (oc + 1) * P, tc * 512:(tc + 1) * 512], so[:], so.k, ssem)
                        return ev

                    def mk_ev_tok(col0):
                        def ev(t16, g0, gw, bk):
                            so, ssem = stgb.next()
                            cp("act" if ecnt[0] % 2 else "dve", so[:], bk[:], [bk.k], [so.k])
                            ecnt[0] += 1
                            store(VT[t16 * P:(t16 + 1) * P, col0 + g0:col0 + g0 + gw], so[:], so.k, ssem)
                        return ev

                    def mk_ev_norm(dst, row0, gcol):
                        def ev(oc, tc, bk):
                            i = ecnt[0] % 2
                            ecnt[0] += 1
                            sq, rs = sqb[i], rsb[i]
                            act(sq[:], bk[:], AF.Square, [bk.k], [sq.k])
                            b2 = nbank()
                            mm(b2[:], ones_bf[:], sq[:], True, True, [ones_bf.k, sq.k], [b2.k])
                            act(rs[:], b2[:], AF.Sqrt, [b2.k], [rs.k], bias=1e-6, scale=1.0 / P)
                            recip(rs[:], rs[:], [rs.k], [rs.k])
                            so, ssem = stgb.next()
                            stt(so[:], bk[:], gcol[:, 0:1], rs[:], ALU.mult, ALU.mult, [bk.k, gcol.k, rs.k], [so.k])
                            store(dst[row0 + oc * P:row0 + (oc + 1) * P, tc * 512:(tc + 1) * 512], so[:], so.k, ssem)
                        return ev

                    proj_fm(cd_w_in, 0, 1536, HM, mk_ev_copy(QT, 0))
                    proj_fm(cd_w_in, 1536, 1536, HM, mk_ev_copy(KT, 0))
                    proj_tm(cd_w_in, 3072, 1536, HM, mk_ev_tok(0))
                    proj_fm(cd_w_in, 4608, 512, HM, mk_ev_norm(QT, 1536, gq))
                    proj_fm(cd_w_in, 5120, 512, HM, mk_ev_norm(KT, 1536, gk))
                    proj_tm(cd_w_in, 5632, 512, HM, mk_ev_tok(1536))
                    kb.barrier()
            if stop_after == "B1":
                return
            sc = 1.0 / math.sqrt(128.0)
            with ExitStack() as ph:
                psb = mk_alloc(ph)
                qr = Ring(kb, psb, 4, [P, S], BF16, "qr", sems=dsems[0:4])
                kr = Ring(kb, psb, 4, [P, S], BF16, "kr", sems=dsems[4:8])
                vr = Ring(kb, psb, 4, [P, 16, P], BF16, "vr1", sems=dsems[8:12])
                mstr_bf = Tile(psb([P, 4, 512], BF16, "mstr_bf"))
                m32 = Tile(psb([P, 4, 512], F32, "m32"))
                kb.dma("sp", m32[:], cst_in[:, MS0:MS0 + 2048].rearrange("p (r t) -> p r t", r=4), misc_sem, writes=[m32.k])
                cp("dve", mstr_bf[:], m32[:], [m32.k], [mstr_bf.k])
                NB = 4
                e32s = [[Tile(psb([P, 512], F32, f"e32{s_}{i}")) for i in range(NB)] for s_ in range(2)]
                spbs = [[Tile(psb([P, 512], BF16, f"spb{s_}{i}")) for i in range(NB)] for s_ in range(2)]
                x32s = [[Tile(psb([P, 512], F32, f"x32{s_}{i}")) for i in range(NB)] for s_ in range(2)]
                wbs = [[Tile(psb([P, 512], BF16, f"wb{s_}{i}")) for i in range(NB)] for s_ in range(2)]
                spacc = [[Tile(psb([P, 512], BF16, f"spacc{s_}{i}")) for i in range(2)] for s_ in range(2)]
                vtv = VT.rearrange("(kt k) e -> k kt e", k=P)
                for hp in range(6):
                    hd2 = []
                    for s_ in range(2):
                        h = 2 * hp + s_
                        QTh, s0 = qr.next()
                        KTh, s1 = kr.next()
                        Vh, s2 = vr.next()
                        kb.dma("sp", QTh[:], QT[h * P:(h + 1) * P, :], s0, writes=[QTh.k])
                        kb.dma("sp", KTh[:], KT[h * P:(h + 1) * P, :], s1, writes=[KTh.k])
                        kb.dma("sp", Vh[:], vtv[:, :, h * P:(h + 1) * P], s2, writes=[Vh.k])
                        hd2.append((h, QTh, KTh, Vh))
                    blocks = [(q, j) for q in range(4) for j in range(4 * q + 3, -1, -1)]
                    bCs = {}

                    def stageA(n, s_):
                        q, j = blocks[n]
                        h, QTh, KTh, Vh = hd2[s_]
                        r = j - 4 * q
                        e32, spb = e32s[s_][n % NB], spbs[s_][n % NB]
                        bZ = nbank6()
                        mm(bZ[:], KTh[:, j * P:(j + 1) * P], QTh[:, q * 512:(q + 1) * 512], True, True, [KTh.k, QTh.k], [bZ.k])
                        act(e32[:], bZ[:], AF.Exp, [bZ.k], [e32.k], scale=sc)
                        act(spb[:], e32[:], AF.Ln, [e32.k], [spb.k], bias=1.0, scale=1.0)
                        if r >= 0:
                            tt("pool", spb[:], spb[:], mstr_bf[:, r, :], ALU.mult, [spb.k, mstr_bf.k], [spb.k])

                    def stageB1(n, s_):
                        q, j = blocks[n]
                        jmax = 4 * q + 3
                        spb, x32 = spbs[s_][n % NB], x32s[s_][n % NB]
                        acc_prev = spacc[s_][(jmax - j) % 2]
                        acc_next = spacc[s_][(jmax - j + 1) % 2]
                        bC = nbank6()
                        if j == jmax:
                            mm(bC[:], trige_bf[:], spb[:], True, True, [trige_bf.k, spb.k], [bC.k])
                        else:
                            mm(bC[:], trige_bf[:], spb[:], True, False, [trige_bf.k, spb.k], [bC.k])
                            mm(bC[:], ones_bf[:], acc_prev[:], False, True, [ones_bf.k, acc_prev.k], [bC.k])
                        act(x32[:], bC[:], AF.Exp, [bC.k], [x32.k], scale=-1.0)
                        if j > 0:
                            if j == jmax:
                                cp("dve", acc_next[:], spb[:], [spb.k], [acc_next.k])
                            else:
                                tt("dve", acc_next[:], acc_prev[:], spb[:], ALU.add, [acc_prev.k, spb.k], [acc_next.k])

                    def stageB2(n, s_):
                        q, j = blocks[n]
                        jmax = 4 * q + 3
                        h, QTh, KTh, Vh = hd2[s_]
                        r = j - 4 * q
                        e32, x32, wb = e32s[s_][n % NB], x32s[s_][n % NB], wbs[s_][n % NB]
                        bO = banks[s_]
                        tt("pool", wb[:], e32[:], x32[:], ALU.mult, [e32.k, x32.k], [wb.k])
                        if r >= 0:
                            tt("pool", wb[:], wb[:], mstr_bf[:, r, :], ALU.mult, [wb.k, mstr_bf.k], [wb.k])
                        mm(bO[:], Vh[:, j, :], wb[:], j == jmax, j == 0, [Vh.k, wb.k], [bO.k])
                        if j == 0:
                            so, ssem = stgb.next()
                            cp("dve", so[:], bO[:], [bO.k], [so.k])
                            store(MIX[h * P:(h + 1) * P, q * 512:(q + 1) * 512], so[:], so.k, ssem)

                    NBK = len(blocks)
                    for n in range(NBK + 2):
                        for s_ in range(2):
                            if n < NBK:
                                stageA(n, s_)
                        for s_ in range(2):
                            if 0 <= n - 1 < NBK:
                                stageB1(n - 1, s_)
                        for s_ in range(2):
                            if 0 <= n - 2 < NBK:
                                stageB2(n - 2, s_)
                kb.barrier()
            if stop_after == "C1":
                return
            with ExitStack() as ph:
                psb = mk_alloc(ph)
                qr = Ring(kb, psb, 2, [P, S], BF16, "qrm", sems=dsems[0:2])
                kr = Ring(kb, psb, 2, [P, S], BF16, "krm", sems=dsems[2:4])
                vr = Ring(kb, psb, 2, [P, 16, P], BF16, "vrm", sems=dsems[4:6])
                gqb = Tile(psb([P, P], F32, "gqb"))
                gkb = Tile(psb([P, P], F32, "gkb"))
                gmx = Tile(psb([P, 2], F32, "gmx"))
                negm = Tile(psb([P, 1], F32, "negm"))
                kb.dma("sp", gqb[:], gq_bc, misc_sem, writes=[gqb.k])
                kb.dma("sp", gkb[:], gk_bc, misc_sem, writes=[gkb.k])
                kb.op("dve", lambda e: e.tensor_reduce(out=gmx[:, 0:1], in_=gqb[:], axis=AX.X, op=ALU.max,
                                                        apply_absolute_value=True), [gqb.k], [gmx.k])
                kb.op("dve", lambda e: e.tensor_reduce(out=gmx[:, 1:2], in_=gkb[:], axis=AX.X, op=ALU.max,
                                                        apply_absolute_value=True), [gkb.k], [gmx.k])
                tt("dve", negm[:], gmx[:, 0:1], gmx[:, 1:2], ALU.mult, [gmx.k], [negm.k])
                ts("dve", negm[:], negm[:], -math.sqrt(128.0), None, ALU.mult, None, [negm.k], [negm.k])
                mincl_bf = Tile(psb([P, 4, 512], BF16, "mincl_bf"))
                m32 = Tile(psb([P, 4, 512], F32, "m32i"))
                kb.dma("sp", m32[:], cst_in[:, MS0 + 2048:MS0 + 4096].rearrange("p (r t) -> p r t", r=4), misc_sem, writes=[m32.k])
                cp("dve", mincl_bf[:], m32[:], [m32.k], [mincl_bf.k])
                Q32 = Tile(psb([P, S], F32, "Q32"))
                K32 = Tile(psb([P, S], F32, "K32"))
                km = Tile(psb([P, 8], F32, "km"))
                gmks = [Tile(psb([P, 8], F32, f"gmk{i}")) for i in range(4)]
                mx8s = [Tile(psb([P, 8], F32, f"mx8{i}")) for i in range(4)]
                sels = [Tile(psb([P, 8], F32, f"sel{i}")) for i in range(4)]
                selT = Tile(psb([8, 4, 512], BF16, "selT"))
                p32s = [Tile(psb([P, 512], F32, f"p32{i}")) for i in range(3)]
                pbs = [Tile(psb([P, 512], BF16, f"pb{i}")) for i in range(3)]
                rd = Tile(psb([P, 512], F32, "rd"))
                vtv = VT.rearrange("(kt k) e -> k kt e", k=P)
                it = 0
                for hd in range(4):
                    row0 = 1536 + hd * P
                    QTh, s0 = qr.next()
                    KTh, s1 = kr.next()
                    Vh, s2 = vr.next()
                    kb.dma("sp", QTh[:], QT[row0:row0 + P, :], s0, writes=[QTh.k])
                    kb.dma("sp", KTh[:], KT[row0:row0 + P, :], s1, writes=[KTh.k])
                    kb.dma("sp", Vh[:], vtv[:, :, row0:row0 + P], s2, writes=[Vh.k])
                    cp("act", Q32[:], QTh[:], [QTh.k], [Q32.k])
                    cp("pool", K32[:], KTh[:], [KTh.k], [K32.k])
                    kb.op("dve", lambda e: e.tensor_reduce(out=km[:], in_=K32[:].rearrange("p (n k) -> p n k", k=256),
                                                            axis=AX.X, op=ALU.add), [K32.k], [km.k])
                    for q in range(4):
                        bSel = banks[0]
                        bG = nbank6()
                        for tl in range(4):
                            t16 = 4 * q + tl
                            mm(bG[:, tl * 8:(tl + 1) * 8], Q32[:, t16 * P:(t16 + 1) * P], km[:], True, True, [Q32.k, km.k], [bG.k])
                        for tl in range(4):
                            t16 = 4 * q + tl
                            gmk1, mx81, sel1 = gmks[tl], mx8s[tl], sels[tl]
                            tt("dve", gmk1[:], bG[:, tl * 8:(tl + 1) * 8], negmask[:, t16, :], ALU.add, [bG.k, cst.k], [gmk1.k])
                            kb.op("dve", lambda e, a=mx81, b_=gmk1: e.max(out=a[:], in_=b_[:]), [gmk1.k], [mx81.k])
                            ts("dve", sel1[:], gmk1[:], mx81[:, 2:3], None, ALU.is_ge, None, [gmk1.k, mx81.k], [sel1.k])
                            tt("dve", sel1[:], sel1[:], validm[:, t16, :], ALU.mult, [sel1.k, cst.k], [sel1.k])
                            tt("dve", sel1[:], sel1[:], ownm[:, t16, :], ALU.add, [sel1.k, cst.k], [sel1.k])
                        for tl in range(4):
                            tp(bSel[0:8, tl * P:(tl + 1) * P], sels[tl][:], [sels[tl].k], [bSel.k])
                        cp("act", selT[:, q, :], bSel[0:8, :], [bSel.k], [selT.k])
                    mblocks = [(q, j) for q in range(4) for j in range(0, 4 * q + 4)]

                    def mA(n):
                        q, j = mblocks[n]
                        nblk = j // 2
                        p32 = p32s[n % 3]
                        bS = nbank6()
                        mm(bS[:], KTh[:, j * P:(j + 1) * P], QTh[:, q * 512:(q + 1) * 512], True, True, [KTh.k, QTh.k], [bS.k])
                        bM = nbank6()
                        mm(bM[:], oh_bf[0:8, nblk * P:(nblk + 1) * P], selT[:, q, :], True, True, [oh_bf.k, selT.k], [bM.k])
                        act(p32[:], bS[:], AF.Exp, [bS.k, negm.k], [p32.k], bias=negm[:, 0:1], scale=sc)
                        bMs[n] = bM

                    def mB(n):
                        q, j = mblocks[n]
                        jmax = 4 * q + 3
                        r = j - 4 * q
                        p32, pb = p32s[n % 3], pbs[n % 3]
                        bM = bMs.pop(n)
                        bO, bD = banks[0], banks[1]
                        tt("dve", pb[:], p32[:], bM[:], ALU.mult, [p32.k, bM.k], [pb.k])
                        if r >= 0:
                            tt("pool", pb[:], pb[:], mincl_bf[:, r, :], ALU.mult, [pb.k, mincl_bf.k], [pb.k])
                        mm(bO[:], Vh[:, j, :], pb[:], j == 0, j == jmax, [Vh.k, pb.k], [bO.k])
                        mm(bD[:], ones_bf[:], pb[:], j == 0, j == jmax, [ones_bf.k, pb.k], [bD.k])
                        if j == jmax:
                            recip(rd[:], bD[:], [bD.k], [rd.k])
                            so, ssem = stgb.next()
                            tt("dve", so[:], bO[:], rd[:], ALU.mult, [bO.k, rd.k], [so.k])
                            store(MIX[row0:row0 + P, q * 512:(q + 1) * 512], so[:], so.k, ssem)

                    bMs = {}
                    NM = len(mblocks)
                    for n in range(NM + 1):
                        if n < NM:
                            mA(n)
                        if n - 1 >= 0:
                            mB(n - 1)
                kb.barrier()
            if stop_after == "D1":
                return
            phase_out_ffn(1, X2, 16, cd_w_out, D, X1, outT)

        layer0()
        if stop_after is None or stop_after.endswith("1"):
            layer1()
        kb.barrier()
    return nc


def _consts():
    i = np.arange(P)
    ident = np.eye(P, dtype=np.float32)
    tri_le = (i[:, None] <= i[None, :]).astype(np.float32)
    sl_gt = (i[:, None] > i[None, :]).astype(np.float32)
    tri_ge = (i[:, None] >= i[None, :]).astype(np.float32)
    ones = np.ones((P, P), np.float32)
    t = np.arange(512)
    mstrict = np.stack([(t[None, :] > (r * P + i[:, None])) for r in range(4)], 1).astype(np.float32)
    mincl = np.stack([(t[None, :] >= (r * P + i[:, None])) for r in range(4)], 1).astype(np.float32)
    n = np.arange(8)
    negmask = np.zeros((P, 16, 8), np.float32)
    valid = np.zeros((P, 16, 8), np.float32)
    own = np.zeros((P, 16, 8), np.float32)
    for t16 in range(16):
        qb = t16 // 2
        negmask[:, t16, :] = np.where(n < qb, 0.0, -1e30)[None, :]
        valid[:, t16, :] = (n < qb).astype(np.float32)[None, :]
        own[:, t16, :] = (n == qb).astype(np.float32)[None, :]
    return np.concatenate([ident, tri_le, sl_gt, tri_ge, ones,
                           negmask.reshape(P, -1), valid.reshape(P, -1), own.reshape(P, -1),
                           mstrict.reshape(P, -1), mincl.reshape(P, -1)], axis=1).astype(np.float32)


def _col(v, nch):
    return np.ascontiguousarray(np.asarray(v, np.float32).reshape(nch, P).T)


def make_in_maps(x, c, norm_mix_g, norm_ffn_g, ada_w, ada_b, ffn_w_gate, ffn_w_up, ffn_w_down,
                 ab_w_in, ab_w_out, gm_ln_g, gm_ln_b, gm_w_s, gm_b_s, ssd_conv_w, ssd_conv_b,
                 ssd_dt_bias, ssd_a_log, ssd_d, ssd_norm_g, cd_w_in, cd_w_out, moba_q_norm_g, moba_k_norm_g):
    f = lambda a: np.ascontiguousarray(np.asarray(a, np.float32))
    bc = lambda v: np.ascontiguousarray(np.broadcast_to(np.asarray(v, np.float32).reshape(1, -1), (P, np.asarray(v).size)))
    shared = {
        "ada_w": f(ada_w),
        "ada_b": np.stack([_col(ada_b[l], 96) for l in range(2)]),
        "gmix": np.stack([_col(norm_mix_g[l], DC) for l in range(2)]),
        "gffn": np.stack([_col(norm_ffn_g[l], DC) for l in range(2)]),
        "ffn_w_gate": f(ffn_w_gate), "ffn_w_up": f(ffn_w_up), "ffn_w_down": f(ffn_w_down),
        "ab_w_in": f(ab_w_in[0]), "ab_w_out": f(ab_w_out[0]),
        "cd_w_in": f(cd_w_in[0]), "cd_w_out": f(cd_w_out[0]),
        "lng": _col(gm_ln_g[0], DC), "lnb": _col(gm_ln_b[0], DC),
        "wsT": np.ascontiguousarray(np.transpose(f(gm_w_s[0]), (2, 0, 1)).reshape(P, DC * P)),
        "bsbc": bc(f(gm_b_s[0]).reshape(-1)),
        "cw": np.ascontiguousarray(np.transpose(f(ssd_conv_w[0]).reshape(4, 24, P), (2, 1, 0)).reshape(P, 96)),
        "cb": _col(ssd_conv_b[0], 24),
        "dtb": bc(ssd_dt_bias[0]), "alog": bc(ssd_a_log[0]),
        "dbc": bc(np.repeat(f(ssd_d[0]), 64)), "sgbc": bc(ssd_norm_g[0]),
        "gq_col": f(moba_q_norm_g[0]).reshape(P, 1).copy(), "gk_col": f(moba_k_norm_g[0]).reshape(P, 1).copy(),
        "gq_bc": bc(moba_q_norm_g[0]), "gk_bc": bc(moba_k_norm_g[0]),
        "cst": _consts(),
    }
    maps = []
    for b in range(8):
        m = dict(shared)
        m["xT"] = np.ascontiguousarray(f(x[b]).T)
        m["c_col"] = _col(c[b], DC)
        maps.append(m)
    return maps


_NC_CACHE = {}


def kernel(**inputs):
    if "nc" not in _NC_CACHE:
        _NC_CACHE["nc"] = build_program()
    nc = _NC_CACHE["nc"]
    in_maps = make_in_maps(**inputs)
    res = run_bass_kernel_spmd(nc, in_maps, core_ids=list(range(8)))
    out = np.stack([np.asarray(r["outT"], dtype=np.float32).T for r in res.results], axis=0)
    return np.ascontiguousarray(out)
```
